# Optimizing a Trainium2 kernel written in Bass

```python
import jax, jax.numpy as jnp
from jax import lax
import numpy as np

D_MODEL = 1024
BATCH = 8
SEQ = 2048
DEPTH = 1

MIX_WIDTH = D_MODEL
HG_WIDTH = MIX_WIDTH // 2
HG_HEAD_DIM = 128
HG_HEADS = HG_WIDTH // HG_HEAD_DIM
GL_WIDTH = MIX_WIDTH - HG_WIDTH
GL_HEADS = 4
GL_DV = GL_WIDTH // GL_HEADS
GL_DK = GL_DV // 2
GL_QK = GL_HEADS * GL_DK
GATE_RANK = 16
GATE_TAU = 16.0
CHUNK = 16
N_GROUPS = 8
EXPERTS_PER_GROUP = 8
N_EXPERTS = N_GROUPS * EXPERTS_PER_GROUP
TOP_K = 2
D_EXPERT = D_MODEL // 2
MOE_BLOCK = 128
DEEPNORM_ALPHA = (2.0 * DEPTH) ** 0.25
DEEPNORM_BETA = (8.0 * DEPTH) ** -0.25
LN_EPS = 1e-5
IN_SPLITS = (HG_WIDTH, HG_WIDTH, HG_WIDTH, HG_WIDTH, GL_QK, GL_QK, GL_WIDTH, GATE_RANK, GL_WIDTH)
IN_COLS = sum(IN_SPLITS)

kernel_name = 'hymba_hgrn2_gla_hmoe_deepnorm'


def layer_norm(x, g, b):
    xf = x.astype(jnp.float32)
    mu = jnp.mean(xf, -1, keepdims=True)
    var = jnp.mean(jnp.square(xf - mu), -1, keepdims=True)
    return ((xf - mu) * lax.rsqrt(var + LN_EPS) * g.astype(jnp.float32) + b.astype(jnp.float32)).astype(x.dtype)


def head_rms_norm(o, gain):
    B, T, H, dv = o.shape
    o = o * lax.rsqrt(jnp.mean(jnp.square(o), -1, keepdims=True) + LN_EPS)
    return o.reshape(B, T, H * dv) * gain.astype(jnp.float32)


def to_chunks(t):
    B, T, H, d = t.shape
    return t.reshape(B, T // CHUNK, CHUNK, H, d).transpose(0, 3, 1, 2, 4)


def chunk_gla(q, k, v, log_g):
    B, T, H, dk = q.shape
    dv = v.shape[-1]
    q, k, v, lg = [to_chunks(t.astype(jnp.float32)) for t in (q, k, v, log_g)]
    b = jnp.cumsum(lg, axis=3)
    b_last = b[..., -1:, :]
    causal = jnp.tril(jnp.ones((CHUNK, CHUNK), bool))[:, :, None]
    diff = b[..., :, None, :] - b[..., None, :, :]
    decay = jnp.exp(jnp.where(causal, diff, -jnp.inf))
    scores = jnp.einsum('bhntsk,bhnsk->bhnts', q[..., :, None, :] * decay, k)
    o_intra = jnp.einsum('bhnts,bhnsv->bhntv', scores, v)
    u = jnp.einsum('bhnsk,bhnsv->bhnkv', k * jnp.exp(b_last - b), v)
    g_chunk = jnp.exp(b_last[..., 0, :])

    def step(S, inp):
        g_n, u_n = inp
        return g_n[..., None] * S + u_n, S

    _, s_prev = lax.scan(step, jnp.zeros((B, H, dk, dv), jnp.float32),
                         (jnp.moveaxis(g_chunk, 2, 0), jnp.moveaxis(u, 2, 0)))
    s_prev = jnp.moveaxis(s_prev, 0, 2)
    o_inter = jnp.einsum('bhntk,bhnkv->bhntv', q * jnp.exp(b), s_prev)
    o = o_intra + o_inter
    return o.transpose(0, 2, 3, 1, 4).reshape(B, T, H, dv)


def hybrid_mixer(x, w_in, w_a2, b_a, lb, norm_h, norm_g, w_out):
    B, T, _ = x.shape
    split_at = [int(i) for i in np.cumsum(IN_SPLITS)[:-1]]
    hq, hf, hi, hg, gq, gk, gv, ga, gg = jnp.split(x @ w_in, split_at, axis=-1)

    f = lb + (1.0 - lb) * jax.nn.sigmoid(hf.astype(jnp.float32))
    o_h = chunk_gla((hq * HG_HEAD_DIM ** -0.5).reshape(B, T, HG_HEADS, HG_HEAD_DIM),
                    (1.0 - f).reshape(B, T, HG_HEADS, HG_HEAD_DIM),
                    hi.reshape(B, T, HG_HEADS, HG_HEAD_DIM),
                    jnp.log(f).reshape(B, T, HG_HEADS, HG_HEAD_DIM))
    o_h = head_rms_norm(o_h, norm_h) * jax.nn.silu(hg.astype(jnp.float32))

    log_a = jax.nn.log_sigmoid((ga @ w_a2 + b_a).astype(jnp.float32)) / GATE_TAU
    o_g = chunk_gla((gq * GL_DK ** -0.5).reshape(B, T, GL_HEADS, GL_DK),
                    gk.reshape(B, T, GL_HEADS, GL_DK),
                    gv.reshape(B, T, GL_HEADS, GL_DV),
                    log_a.reshape(B, T, GL_HEADS, GL_DK))
    o_g = head_rms_norm(o_g, norm_g) * jax.nn.silu(gg.astype(jnp.float32))

    o = jnp.concatenate([o_h, o_g], axis=-1).astype(x.dtype)
    return o @ w_out


def hierarchical_moe(x, w_group_router, w_expert_router, w_gate, w_up, w_down):
    B, T, D = x.shape
    n_tok = B * T
    xf = x.reshape(n_tok, D)
    p_group = jax.nn.softmax((xf @ w_group_router).astype(jnp.float32), axis=-1)
    p_top_group, g_idx = lax.top_k(p_group, 1)
    e_logits = (xf @ w_expert_router).astype(jnp.float32).reshape(n_tok, N_GROUPS, EXPERTS_PER_GROUP)
    e_logits = jnp.take_along_axis(e_logits, g_idx[:, :, None], axis=1)[:, 0]
    p_top, e_local = lax.top_k(jax.nn.softmax(e_logits, axis=-1), TOP_K)
    gates = p_top_group * p_top / jnp.sum(p_top, -1, keepdims=True)
    expert_id = g_idx * EXPERTS_PER_GROUP + e_local

    n_assign = n_tok * TOP_K
    flat_e = expert_id.reshape(-1)
    flat_tok = jnp.repeat(jnp.arange(n_tok, dtype=jnp.int32), TOP_K)
    flat_w = gates.reshape(-1)
    order = jnp.argsort(flat_e)
    sorted_e = flat_e[order]
    counts = jnp.bincount(flat_e, length=N_EXPERTS)
    starts = jnp.cumsum(counts) - counts
    padded = (counts + MOE_BLOCK - 1) // MOE_BLOCK * MOE_BLOCK
    padded_ends = jnp.cumsum(padded)
    padded_starts = padded_ends - padded
    dest = padded_starts[sorted_e] + jnp.arange(n_assign, dtype=jnp.int32) - starts[sorted_e]
    n_blocks = -(-n_assign // MOE_BLOCK) + N_EXPERTS
    n_slots = n_blocks * MOE_BLOCK
    slot_tok = jnp.full((n_slots,), n_tok, jnp.int32).at[dest].set(flat_tok[order])
    slot_w = jnp.zeros((n_slots,), jnp.float32).at[dest].set(flat_w[order])
    block_expert = jnp.minimum(
        jnp.searchsorted(padded_ends, jnp.arange(n_blocks, dtype=jnp.int32) * MOE_BLOCK, side='right'),
        N_EXPERTS - 1)
    x_pad = jnp.concatenate([xf, jnp.zeros((1, D), xf.dtype)], axis=0)
    xb = x_pad[slot_tok].reshape(n_blocks, MOE_BLOCK, D)

    def expert_block(args):
        xb_, e = args
        h = jax.nn.silu(xb_ @ w_gate[e]) * (xb_ @ w_up[e])
        return h @ w_down[e]

    yb = lax.map(expert_block, (xb, block_expert)).reshape(n_slots, D)
    y = jnp.zeros((n_tok + 1, D), jnp.float32).at[slot_tok].add(yb.astype(jnp.float32) * slot_w[:, None])
    return y[:n_tok].astype(x.dtype).reshape(B, T, D)


def setup_inputs(seed: int = 0) -> dict:
    key = jax.random.key(seed)
    ks = jax.random.split(key, 20)
    L = DEPTH
    f32 = jnp.float32

    def nrm(k, shape, scale):
        return jax.random.normal(k, shape, f32) * scale

    col_scale = jnp.concatenate([jnp.full((n,), DEEPNORM_BETA if i in (2, 6) else 1.0, f32)
                                 for i, n in enumerate(IN_SPLITS)])
    return {
        'x': nrm(ks[0], (BATCH, SEQ, D_MODEL), 1.0),
        'w_in': nrm(ks[1], (L, D_MODEL, IN_COLS), D_MODEL ** -0.5) * col_scale,
        'w_a2': nrm(ks[2], (L, GATE_RANK, GL_QK), GATE_RANK ** -0.5),
        'b_a': nrm(ks[3], (L, GL_QK), 0.1),
        'lb_logits': nrm(ks[4], (L + 1, HG_WIDTH), 0.5),
        'norm_h': 1.0 + nrm(ks[5], (L, HG_WIDTH), 0.02),
        'norm_g': 1.0 + nrm(ks[6], (L, GL_WIDTH), 0.02),
        'w_out': nrm(ks[7], (L, MIX_WIDTH, D_MODEL), MIX_WIDTH ** -0.5 * DEEPNORM_BETA),
        'ln1_g': 1.0 + nrm(ks[8], (L, D_MODEL), 0.02),
        'ln1_b': nrm(ks[9], (L, D_MODEL), 0.02),
        'w_group_router': nrm(ks[10], (L, D_MODEL, N_GROUPS), D_MODEL ** -0.5),
        'w_expert_router': nrm(ks[11], (L, D_MODEL, N_EXPERTS), D_MODEL ** -0.5),
        'w_gate': nrm(ks[12], (L, N_EXPERTS, D_MODEL, D_EXPERT), D_MODEL ** -0.5 * DEEPNORM_BETA),
        'w_up': nrm(ks[13], (L, N_EXPERTS, D_MODEL, D_EXPERT), D_MODEL ** -0.5 * DEEPNORM_BETA),
        'w_down': nrm(ks[14], (L, N_EXPERTS, D_EXPERT, D_MODEL), D_EXPERT ** -0.5 * DEEPNORM_BETA),
        'ln2_g': 1.0 + nrm(ks[15], (L, D_MODEL), 0.02),
        'ln2_b': nrm(ks[16], (L, D_MODEL), 0.02),
    }


def reference(x, w_in, w_a2, b_a, lb_logits, norm_h, norm_g, w_out, ln1_g, ln1_b,
              w_group_router, w_expert_router, w_gate, w_up, w_down, ln2_g, ln2_b):
    lb_all = jnp.cumsum(jax.nn.softmax(lb_logits.astype(jnp.float32), axis=0), axis=0)
    for l in range(DEPTH):
        h = hybrid_mixer(x, w_in[l], w_a2[l], b_a[l], lb_all[l], norm_h[l], norm_g[l], w_out[l])
        x = layer_norm(DEEPNORM_ALPHA * x + h, ln1_g[l], ln1_b[l])
        h = hierarchical_moe(x, w_group_router[l], w_expert_router[l], w_gate[l], w_up[l], w_down[l])
        x = layer_norm(DEEPNORM_ALPHA * x + h, ln2_g[l], ln2_b[l])
    return x
```

```python
import numpy as np
from contextlib import ExitStack
import concourse.bass as bass
import concourse.mybir as mybir
from concourse.bass_utils import run_bass_kernel_spmd

F32 = mybir.dt.float32
BF16 = mybir.dt.bfloat16
I32 = mybir.dt.int32
AF = mybir.ActivationFunctionType
ALU = mybir.AluOpType
AX = mybir.AxisListType

NCORES = 8
T = 2048
D = 1024
NT = T // 128
ALPHA = 2.0 ** 0.25
EPS = 1e-5
NE = 64
CAP = 128
NSLOT = NE * CAP
WCOLS = 3840

ENG_NAMES = ("pe", "act", "dve", "pool", "sp")


class Op:
    __slots__ = ("eng", "fn", "reads", "writes", "dma_key", "idx", "deps", "signal",
                 "sem", "val", "waits", "dma_bytes")

    def __init__(self, eng, fn, reads, writes, dma_key):
        self.eng = eng
        self.fn = fn
        self.reads = tuple(reads)
        self.writes = tuple(writes)
        self.dma_key = dma_key
        self.deps = set()
        self.signal = False
        self.sem = None
        self.val = 0
        self.waits = []


_EST = [False]


class _Ins:
    def then_inc(self, *a, **k):
        return self


class _Rec:
    def __init__(self):
        self.calls = []

    def __getattr__(self, name):
        def f(*a, **k):
            self.calls.append((name, a, k))
            return _Ins()
        return f


def _fsize(ap):
    n = 1
    for d in list(ap.shape)[1:]:
        n *= int(d)
    return n


def _dsz(ap):
    return int(mybir.dt.size(ap.dtype))


class Prog:
    def __init__(self):
        self.ops = []

    def op(self, eng, fn, reads=(), writes=()):
        o = Op(eng, fn, reads, writes, None)
        o.idx = len(self.ops)
        self.ops.append(o)
        return o

    def dma(self, queue, fn, reads=(), writes=(), key=None):
        o = Op(queue, fn, reads, writes, key)
        o.idx = len(self.ops)
        self.ops.append(o)
        return o

    def analyze(self, final_wait_keys=(), do_schedule=False):
        writers, readers = {}, {}
        for o in self.ops:
            deps = set()
            for b in o.reads:
                deps.update(writers.get(b, ()))
            for b in o.writes:
                deps.update(writers.get(b, ()))
                deps.update(readers.get(b, ()))
            deps.discard(o.idx)
            o.deps = deps
            for b in o.reads:
                readers.setdefault(b, []).append(o.idx)
            for b in o.writes:
                writers[b] = [o.idx]
                readers[b] = []
        self.final_deps = set()
        for b in final_wait_keys:
            self.final_deps.update(writers.get(b, ()))
        if do_schedule:
            self.schedule()
        for o in self.ops:
            for d in o.deps:
                self.ops[d].signal = True
        for d in self.final_deps:
            self.ops[d].signal = True
        cnt = {}
        for o in self.ops:
            if o.dma_key is not None:
                s = ("dma", o.dma_key)
                o.signal = True
                cnt[s] = cnt.get(s, 0) + 16
                o.sem, o.val = s, cnt[s]
            elif o.signal:
                s = ("eng", o.eng)
                cnt[s] = cnt.get(s, 0) + 1
                o.sem, o.val = s, cnt[s]
        self.sem_names = sorted(cnt.keys(), key=str)
        seen = {e: {} for e in ENG_NAMES}
        know = {}
        nw = 0
        for o in self.ops:
            sn = seen[o.eng]
            need = {}
            for d in o.deps:
                do = self.ops[d]
                if sn.get(do.sem, 0) >= do.val:
                    continue
                if need.get(do.sem, 0) < do.val:
                    need[do.sem] = do.val
            for d in o.deps:
                do = self.ops[d]
                if do.sem in need:
                    for s, v in know[d].items():
                        if sn.get(s, 0) < v:
                            sn[s] = v
            o.waits = sorted(need.items(), key=str)
            nw += len(o.waits)
            if o.signal:
                k = dict(sn)
                k[o.sem] = max(k.get(o.sem, 0), o.val)
                know[o.idx] = k
        self.final_waits = {}
        for d in self.final_deps:
            do = self.ops[d]
            if self.final_waits.get(do.sem, 0) < do.val:
                self.final_waits[do.sem] = do.val
        for s, v in cnt.items():
            if s[0] == "dma" and self.final_waits.get(s, 0) < v:
                self.final_waits[s] = v
        self.n_waits = nw
        self.max_cnt = cnt


    def estimate(self, o):
        rec = _Rec()
        _EST[0] = True
        try:
            o.fn(rec)
        finally:
            _EST[0] = False
        eng_t, lat = 0.0, 0.0
        for name, a, k in rec.calls:
            def arg(nm, pos):
                return k[nm] if nm in k else (a[pos] if len(a) > pos else None)
            if name == "matmul":
                rhs = arg("rhs", 2)
                n = _fsize(rhs)
                f = 4.0 if _dsz(rhs) == 4 else 1.0
                eng_t += 0.06 + f * max(n, 64) * 0.00042
            elif name == "transpose":
                eng_t += 0.12
            elif name in ("dma_start", "indirect_dma_start"):
                out = arg("out", 0)
                inn = k.get("in_", None)
                byts = _fsize(out) * out.shape[0] * _dsz(out)
                if inn is not None and hasattr(inn, "shape"):
                    byts = min(byts, _fsize(inn) * inn.shape[0] * _dsz(inn))
                eng_t += 0.8 if o.eng == "pool" else 0.12
                lat = max(lat, 2.2 + byts / (90e3 if name == "indirect_dma_start" else 300e3))
                o.dma_bytes = byts
            elif name == "to_reg":
                pass
            else:
                src = None
                for cnd in [k.get("in_"), k.get("in0"), k.get("data"), k.get("out")] + list(a):
                    if cnd is not None and hasattr(cnd, "shape"):
                        src = cnd
                        break
                n = _fsize(src) if src is not None else 64
                if o.eng == "act":
                    eng_t += 0.22 + n / 900.0
                elif o.eng == "pool":
                    eng_t += 0.35 + n / 330.0
                else:
                    eng_t += 0.12 + n / 800.0
        return eng_t, lat

    def schedule(self):
        n = len(self.ops)
        cost, lat = [0.0] * n, [0.0] * n
        for o in self.ops:
            o.dma_bytes = 0
            c, l = self.estimate(o)
            cost[o.idx], lat[o.idx] = c, l
        users = [[] for _ in range(n)]
        for o in self.ops:
            for d in o.deps:
                users[d].append(o.idx)
        prio = [0.0] * n
        for i in range(n - 1, -1, -1):
            m = 0.0
            for u in users[i]:
                if prio[u] > m:
                    m = prio[u]
            prio[i] = cost[i] + lat[i] + m
        ndep = [len(o.deps) for o in self.ops]
        est = [0.0] * n
        fin = [0.0] * n
        start = [0.0] * n
        t_eng = {e: 0.0 for e in ENG_NAMES}
        cand = {e: [] for e in ENG_NAMES}
        for o in self.ops:
            if ndep[o.idx] == 0:
                cand[o.eng].append(o.idx)
        dma_free = 0.0
        done = 0
        WINDOW = 0.4
        while done < n:
            best = None
            for e in ENG_NAMES:
                cl = cand[e]
                if not cl:
                    continue
                te = t_eng[e]
                ms = min(max(te, est[i]) for i in cl)
                pick = None
                for i in cl:
                    st_ = max(te, est[i])
                    if st_ <= ms + WINDOW:
                        key = (-prio[i], i)
                        if pick is None or key < pick[0]:
                            pick = (key, i, st_)
                if best is None or pick[2] < best[2]:
                    best = (e, pick[1], pick[2])
            e, i, st_ = best
            cand[e].remove(i)
            start[i] = st_
            t_eng[e] = st_ + cost[i]
            if lat[i] > 0:
                byts = self.ops[i].dma_bytes
                if byts >= (1 << 20):
                    b0 = max(st_ + cost[i], dma_free)
                    fin[i] = b0 + lat[i]
                    dma_free = fin[i] - 2.2
                else:
                    fin[i] = st_ + cost[i] + lat[i]
            else:
                fin[i] = st_ + cost[i]
            done += 1
            for u in users[i]:
                if fin[i] > est[u]:
                    est[u] = fin[i]
                ndep[u] -= 1
                if ndep[u] == 0:
                    cand[self.ops[u].eng].append(u)
        order = sorted(range(n), key=lambda i: (start[i], i))
        remap = {old: new for new, old in enumerate(order)}
        new_ops = [self.ops[i] for i in order]
        for o in new_ops:
            o.deps = set(remap[d] for d in o.deps)
        for new, o in enumerate(new_ops):
            o.idx = new
        self.final_deps = set(remap[d] for d in self.final_deps)
        self.ops = new_ops
        self.sched_len = max(fin) if n else 0.0

    def emit(self, nc, stack):
        sems = {}
        for i, s in enumerate(self.sem_names):
            sems[s] = stack.enter_context(nc.semaphore("s%d" % i))
        block = stack.enter_context(nc.Block())
        per = {e: [o for o in self.ops if o.eng == e] for e in ENG_NAMES}
        final_waits = self.final_waits

        def run(e, ename):
            for o in per[ename]:
                for s, v in o.waits:
                    e.wait_ge(sems[s], v)
                ins = o.fn(e)
                if o.signal:
                    ins.then_inc(sems[o.sem], 16 if o.dma_key is not None else 1)
            if ename == "sp":
                for s, v in sorted(final_waits.items(), key=str):
                    e.wait_ge(sems[s], v)

        @block.tensor
        def _(e):
            run(e, "pe")

        @block.scalar
        def _(e):
            run(e, "act")

        @block.vector
        def _(e):
            run(e, "dve")

        @block.gpsimd
        def _(e):
            run(e, "pool")

        @block.sync
        def _(e):
            run(e, "sp")


C_ID, C_L, C_U, C_LG, C_UG, C_SL, C_ONE, C_MASK, C_IND, C_EB = 0, 128, 256, 384, 512, 640, 768, 896, 1408, 1412
C_CB = 1476
C_TOT = 1478


def make_consts():
    c = np.zeros((128, C_TOT), np.float32)
    s = np.arange(128)[:, None]
    t = np.arange(128)[None, :]
    same = (s // 64) == (t // 64)
    c[:, C_ID:C_ID + 128] = np.eye(128)
    L = ((s <= t) & same).astype(np.float32)
    U = ((s > t) & same).astype(np.float32)
    c[:, C_L:C_L + 128] = L
    c[:, C_U:C_U + 128] = U
    c[:, C_LG:C_LG + 128] = -L / 16.0
    c[:, C_UG:C_UG + 128] = -U / 16.0
    c[:, C_SL:C_SL + 128] = (s < t).astype(np.float32)
    c[:, C_ONE:C_ONE + 128] = 1.0
    tt = np.arange(64)[None, :]
    m = ((s % 64) <= tt).astype(np.float32)
    c[:, C_MASK:C_MASK + 512] = np.tile(m, (1, 8))
    ind = (np.arange(128)[:, None] // 64 == np.arange(2)[None, :]).astype(np.float32)
    c[:, C_IND:C_IND + 2] = ind
    c[:, C_IND + 2:C_IND + 4] = -ind / 16.0
    c[:, C_EB:C_EB + 64] = (np.arange(64) * CAP)[None, :]
    c[:, C_CB] = np.where(np.arange(128) < 64, 0.0, -30000.0)
    c[:, C_CB + 1] = np.where(np.arange(128) >= 64, 0.0, -30000.0)
    return c


def build_program(stage="full"):
    nc = bass.Bass("TRN2", target_bir_lowering=False)
    P = Prog()

    def din(name, shape, dt=F32):
        return nc.dram_tensor(name, list(shape), dt, kind="ExternalInput").ap()

    x_d = din("x", [T, D])
    xT_d = din("xT", [NT, 128, 1024])
    w_in_d = din("w_in_r", [D, 3584])
    w_gaT_d = din("w_gaT", [16, D])
    w_a2_d = din("w_a2", [16, 256])
    b_a_d = din("b_a", [1, 256])
    lbl_d = din("lb_logits", [2, 512])
    gain_d = din("norm_hg", [1, 1024])
    w_out_d = din("w_out", [D, D])
    ln1g_d = din("ln1_g", [1, D])
    ln1b_d = din("ln1_b", [1, D])
    wr_d = din("w_router", [D, 72])
    if stage != "A":
        wgu_d = din("w_gu", [NE, 128, 8192])
        wdn_d = din("w_dn", [NE, 128, 4096])
    ln2g_d = din("ln2_g", [1, D])
    ln2b_d = din("ln2_b", [1, D])
    cst_d = din("cst", [128, C_TOT])
    zeros_d = din("zeros_bf", [1024, D], BF16)
    out_d = nc.dram_tensor("out", [T, D], F32, kind="ExternalOutput").ap()
    X1_d = nc.dram_tensor("X1s", [T, D], F32, kind="Internal").ap()
    xb_d = nc.dram_tensor("xbs", [NSLOT, D], BF16, kind="Internal").ap()
    X1b_d = nc.dram_tensor("X1bs", [T, D], BF16, kind="Internal").ap()
    yb_d = nc.dram_tensor("ybs", [NSLOT, D], F32, kind="Internal").ap()

    _breg = {}

    def breg(e):
        if _EST[0]:
            return 0
        if "r" not in _breg:
            _breg["r"] = e.to_reg(NSLOT - 1)
        return _breg["r"]

    st = ExitStack()
    with st:
        def sb(name, shape, dt=F32):
            return st.enter_context(nc.sbuf_tensor("sb_" + name, list(shape), dt))

        def ps(name):
            return st.enter_context(nc.psum_tensor(name, [128, 512], F32))

        PB = [ps("pb0"), ps("pb1")]
        pc2, pc3, ptr, psc, po6, po7 = ps("pc2"), ps("pc3"), ps("ptr"), ps("psc"), ps("po6"), ps("po7")
        PO = [po6, po7]
        ptr_bf = ptr[:].bitcast(BF16)

        cst = sb("cst", [128, C_TOT])
        ident_bf = sb("ident_bf", [128, 128], BF16)
        w_all = sb("w_all", [128, 8, WCOLS], BF16)
        w_out = sb("w_out_bf", [128, 8, D], BF16)
        wr = sb("wr", [128, 8, 72])
        stg = [sb("stg0", [128, 2048]), sb("stg1", [128, 2048])]
        lb = sb("lb", [128, 512])
        oml = sb("oml", [128, 512])
        lbl = sb("lbl", [128, 2, 512])
        gain = sb("gain", [128, 1024])
        ln1g = sb("ln1g", [128, D])
        ln1b = sb("ln1b", [128, D])
        b_a = sb("b_a", [128, 256])
        wgaT = sb("wgaT", [128, D])
        wa2 = sb("wa2", [16, 256])
        LG = sb("LG", [128, NT, 72])

        ident = cst[:, C_ID:C_ID + 128]

        LV = SETUP_LEVEL
        P.dma("sp", lambda e: e.dma_start(out=cst[:], in_=cst_d), writes=["cst"], key="cst")
        P.op("dve", lambda e: e.tensor_copy(out=ident_bf[:], in_=ident), reads=["cst"], writes=["ident_bf"])

        def bload(dst, src, n, key):
            P.dma("sp", lambda e: e.dma_start(out=dst[:], in_=src[0:1, :].to_broadcast([128, n])),
                  writes=[key], key=key)

        if LV >= 2:
          bload(gain, gain_d, 1024, "gain")
        bload(ln1g, ln1g_d, D, "ln1g")
        bload(ln1b, ln1b_d, D, "ln1b")
        bload(b_a, b_a_d, 256, "b_a")
        P.dma("sp", lambda e: e.dma_start(out=lbl[:, 0, :], in_=lbl_d[0:1, :].to_broadcast([128, 512])),
              writes=["lbl0"], key="lbl0")
        P.dma("sp", lambda e: e.dma_start(out=lbl[:, 1, :], in_=lbl_d[1:2, :].to_broadcast([128, 512])),
              writes=["lbl1"], key="lbl1")
        P.dma("sp", lambda e: e.dma_start(out=wgaT[0:16, :], in_=w_gaT_d), writes=["wgaT"], key="wgaT")
        P.dma("sp", lambda e: e.dma_start(out=wa2[:], in_=w_a2_d), writes=["wa2"], key="wa2")
        P.dma("sp", lambda e: e.dma_start(out=wr[:], in_=wr_d.rearrange("(k p) n -> p k n", p=128)),
              writes=["wr"], key="wr")
        P.op("dve", lambda e: e.tensor_tensor(out=lb[:], in0=lbl[:, 1, :], in1=lbl[:, 0, :], op=ALU.subtract),
             reads=["lbl0", "lbl1"], writes=["lb"])
        P.op("act", lambda e: e.activation(out=lb[:], in_=lb[:], func=AF.Exp), reads=["lb"], writes=["lb"])
        P.op("dve", lambda e: e.tensor_scalar(out=lb[:], in0=lb[:], scalar1=1.0, scalar2=None, op0=ALU.add),
             reads=["lb"], writes=["lb"])
        P.op("dve", lambda e: e.reciprocal(out=lb[:], in_=lb[:]), reads=["lb"], writes=["lb"])
        P.op("dve", lambda e: e.tensor_scalar(out=oml[:], in0=lb[:], scalar1=-1.0, scalar2=1.0,
                                              op0=ALU.mult, op1=ALU.add), reads=["lb"], writes=["oml"])

        si = 0
        for k in range(8 if LV >= 4 else 0):
            for hf_ in range(2):
                sg = stg[si % 2]
                skey = "stg%d" % (si % 2)
                c0 = hf_ * 1792
                P.dma("sp", (lambda sg=sg, k=k, c0=c0: lambda e: e.dma_start(
                    out=sg[:, 0:1792], in_=w_in_d[k * 128:(k + 1) * 128, c0:c0 + 1792]))(),
                    writes=[skey], key=skey)
                eng = "pool" if si % 2 == 0 else "act"
                if hf_ == 0:
                    def fn(e, sg=sg, k=k, eng=eng):
                        cp = e.tensor_copy if eng == "pool" else (lambda out, in_: e.activation(out=out, in_=in_, func=AF.Copy))
                        cp(out=w_all[:, k, 0:512], in_=sg[:, 0:512])
                        return cp(out=w_all[:, k, 768:2048], in_=sg[:, 512:1792])
                else:
                    def fn(e, sg=sg, k=k, eng=eng):
                        cp = e.tensor_copy if eng == "pool" else (lambda out, in_: e.activation(out=out, in_=in_, func=AF.Copy))
                        return cp(out=w_all[:, k, 2048:3840], in_=sg[:, 0:1792])
                P.op(eng, fn, reads=[skey], writes=["w_all_%d_%d" % (k, hf_)])
                si += 1
        WALL_KEYS = ["w_all_%d_%d" % (k, h) for k in range(8) for h in range(2)] + ["w_all_z"]
        for k in range(8 if LV >= 5 else 0):
            sg = stg[si % 2]
            skey = "stg%d" % (si % 2)
            P.dma("sp", (lambda sg=sg, k=k: lambda e: e.dma_start(
                out=sg[:, 0:1024], in_=w_out_d[k * 128:(k + 1) * 128, :]))(), writes=[skey], key=skey)
            eng = "pool" if si % 2 == 0 else "act"

            def fn(e, sg=sg, k=k, eng=eng):
                cp = e.tensor_copy if eng == "pool" else (lambda out, in_: e.activation(out=out, in_=in_, func=AF.Copy))
                return cp(out=w_out[:, k, :], in_=sg[:, 0:1024])
            P.op(eng, fn, reads=[skey], writes=["w_out_%d" % k])
            si += 1
        WOUT_KEYS = ["w_out_%d" % k for k in range(8)]
        for k2 in range(4 if LV >= 6 else 0):
            bank = PB[k2 % 2]
            bkey = "pb%d" % (k2 % 2)

            def fn(e, bank=bank, k2=k2):
                ins = None
                for j in range(2):
                    k = 2 * k2 + j
                    ins = e.matmul(bank[:, 256 * j:256 * j + 256], lhsT=wgaT[0:16, k * 128:(k + 1) * 128],
                                   rhs=wa2[:, :], start=True, stop=True)
                return ins
            P.op("pe", fn, reads=["wgaT", "wa2"], writes=[bkey])

            def fn2(e, bank=bank, k2=k2):
                ins = None
                for j in range(2):
                    k = 2 * k2 + j
                    ins = e.tensor_copy(out=w_all[:, k, 512:768], in_=bank[:, 256 * j:256 * j + 256])
                return ins
            P.op("dve", fn2, reads=[bkey], writes=["w_all_z%d" % k2])
        WALL_KEYS = ["w_all_%d_%d" % (k, h) for k in range(8) for h in range(2)] + ["w_all_z%d" % k for k in range(4)]

        xTs = sb("xTs", [128, 8, 128])
        xTb = sb("xTb", [128, 8, 128], BF16)
        xt = sb("xt", [128, D])
        sig = sb("sig", [128, 512])
        omf = sb("omf", [128, 512])
        lgh = sb("lgh", [128, 512])
        zb = sb("zb", [128, 256])
        gkf = sb("gkf", [128, 256])
        spg = sb("spg", [128, 256])
        eb = sb("eb", [128, 768])
        enb = sb("enb", [128, 768])
        erb = sb("erb", [128, 2, 768], BF16)
        gdec = sb("gdec", [128, 12])
        qk = sb("qk", [128, 1792], BF16)
        kh = sb("kh", [128, 2, 768], BF16)
        vv = sb("vv", [128, 1024], BF16)
        sil = sb("sil", [128, 1024])
        qkT = sb("qkT", [128, 14, 128], BF16)
        ATf = sb("ATf", [128, 8, 128], BF16)
        Sf = sb("Sf", [128, 6, 128])
        Sb = sb("Sb", [128, 6, 128], BF16)
        sqj = sb("sqj", [128, 8, 128], BF16)
        ss = sb("ss", [128, 8])
        rstd = sb("rstd", [128, 8])
        on = sb("on", [128, 1024], BF16)
        onT = sb("onT", [128, 8, 128], BF16)
        yy = sb("yy", [128, D])
        bst = sb("bst", [128, 12])
        mv = sb("mv", [128, 2])
        rs1 = sb("rs1", [128, 1])
        x1 = yy
        x1T = sb("x1T", [128, 8, 128])

        if NTILES < NT:
            P.op("dve", lambda e: e.memset(LG[:], 0.0), writes=["LG_%d" % i for i in range(NT)])
        P.op("dve", lambda e: e.memset(Sf[:], 0.0), writes=["Sf"])
        P.op("pool", lambda e: e.memset(qk[:], 0.0), writes=["qk_zero"])
        P.op("pool", lambda e: e.memset(ATf[:], 0.0), writes=["AT_zero"])
        cbias = cst[:, C_CB:C_CB + 2]
        P.op("pool", lambda e: e.memset(Sb[:], 0.0), writes=["Sb"])

        GROUPS = [(0, 512), (512, 1024), (1024, 1536), (1536, 1792), (1792, 2304), (2304, 2816),
                  (2816, 3328), (3328, 3840)]
        maskrep = cst[:, C_MASK:C_MASK + 512]
        L64 = cst[:, C_L:C_L + 128]
        U64 = cst[:, C_U:C_U + 128]
        LGm = cst[:, C_LG:C_LG + 128]
        UGm = cst[:, C_UG:C_UG + 128]
        ind = cst[:, C_IND:C_IND + 4]

        gcount = [0]

        def proj_group(g):
            c0, c1 = GROUPS[g]
            bi = gcount[0] % 2
            gcount[0] += 1
            bank = PB[bi]

            def fn(e):
                ins = None
                for k in range(8):
                    ins = e.matmul(bank[:, 0:c1 - c0], lhsT=xTb[:, k, :], rhs=w_all[:, k, c0:c1],
                                   start=(k == 0), stop=(k == 7))
                return ins
            P.op("pe", fn, reads=["xTb"] + WALL_KEYS, writes=["pb%d" % bi])
            return bank, "pb%d" % bi

        s0bA = stg[0][:].bitcast(BF16)
        QKT_BUF = [qkT, s0bA[:, 0:1792].rearrange("p (a b) -> p a b", b=128)]
        ATF_BUF = [ATf, s0bA[:, 1792:2816].rearrange("p (a b) -> p a b", b=128)]
        VV_BUF = [vv, s0bA[:, 2816:3840]]
        SIL_BUF = [sil, stg[1][:, 0:1024]]
        KH_BUF = [kh, stg[1][:, 1024:1792].bitcast(BF16).rearrange("p (a b) -> p a b", b=768)]
        GD_BUF = [gdec, stg[1][:, 1792:1804]]
        XT_BUF = [xt, wgaT]
        STG0_B = ["qkT_a_B", "qkT_b_B", "AT_B", "AT_zero_B", "vv_h_B", "vv_g_B"]
        STG1_B = ["sil_h_B", "sil_g_B", "gg_B", "kh_h0_B", "kh_h1_B", "kh_g0_B", "kh_g1_B", "gdec_B"]
        WGAT_B = ["xt_B"]
        P.op("pool", lambda e: e.memset(stg[0][:], 0.0), writes=["stg0"] + STG0_B)
        P.op("pool", lambda e: e.memset(stg[1][:], 0.0), writes=["stg1"] + STG1_B)
        P.op("pool", lambda e: e.memset(wgaT[:], 0.0), writes=["wgaT"] + WGAT_B + ["w_all_z%d" % k for k in range(4)])
        def ln_tail(src, skey, dst, dkey, gt, bt, gk_, bk_):
            def fn(e):
                e.bn_stats(out=bst[:, 0:6], in_=src[:, 0:512])
                return e.bn_stats(out=bst[:, 6:12], in_=src[:, 512:1024])
            P.op("dve", fn, reads=[skey], writes=["bst"])
            P.op("dve", lambda e: e.bn_aggr(out=mv[:], in_=bst[:]), reads=["bst"], writes=["mv"])
            P.op("act", lambda e: e.activation(out=rs1[:], in_=mv[:, 1:2], func=AF.Ln, bias=EPS),
                 reads=["mv"], writes=["rs1"])
            P.op("act", lambda e: e.activation(out=rs1[:], in_=rs1[:], func=AF.Exp, scale=-0.5),
                 reads=["rs1"], writes=["rs1"])
            P.op("dve", lambda e: e.tensor_scalar(out=src[:], in0=src[:], scalar1=mv[:, 0:1], scalar2=rs1[:, 0:1],
                                                  op0=ALU.subtract, op1=ALU.mult),
                 reads=[skey, "mv", "rs1"], writes=[skey])
            P.op("dve", lambda e: e.tensor_tensor(out=dst[:], in0=src[:], in1=gt[:], op=ALU.mult),
                 reads=[skey, gk_], writes=[dkey])
            P.op("pool", lambda e: e.tensor_tensor(out=dst[:], in0=dst[:], in1=bt[:], op=ALU.add),
                 reads=[dkey, bk_], writes=[dkey])
        PO_KEYS = ["po6_0", "po6_1", "po7_0", "po7_1"]
        def front(i):
            pr = i % 2
            sfx = "" if pr == 0 else "_B"
            qkT, ATf, vv, kh, gdec, sil, xt = QKT_BUF[pr], ATF_BUF[pr], VV_BUF[pr], KH_BUF[pr], GD_BUF[pr], SIL_BUF[pr], XT_BUF[pr]
            t0 = i * 128
            t0 = i * 128
            P.dma("pool", (lambda i=i: lambda e: e.dma_start(
                out=xTb[:], in_=xT_d[i].rearrange("p (k t) -> p k t", k=8)))(),
                writes=["xTb"], key="xTb")
            P.dma("sp", (lambda t0=t0: lambda e: e.dma_start(out=xt[:], in_=x_d[t0:t0 + 128, :]))(),
                  writes=[("xt" + sfx)], key=("xt" + sfx))
            if BODY_LEVEL < -2:
                return
            bank, bk = proj_group(0)
            P.op("act", (lambda bank=bank: lambda e: e.activation(out=sig[:], in_=bank[:, 0:512], func=AF.Sigmoid))(),
                 reads=[bk], writes=["sig"])
            P.op("dve", lambda e: e.tensor_tensor(out=sig[:], in0=sig[:], in1=oml[:], op=ALU.mult),
                 reads=["sig", "oml"], writes=["sig"])
            P.op("pool", lambda e: e.tensor_tensor(out=lgh[:], in0=sig[:], in1=lb[:], op=ALU.add),
                 reads=["sig", "lb"], writes=["lgh"])
            P.op("pool", lambda e: e.tensor_tensor(out=omf[:], in0=oml[:], in1=sig[:], op=ALU.subtract),
                 reads=["sig", "oml"], writes=["omf"])
            if BODY_LEVEL < -1:
                return
            bank, bk = proj_group(1)
            P.op("dve", (lambda bank=bank: lambda e: e.tensor_tensor(out=zb[:], in0=bank[:, 0:256], in1=b_a[:],
                                                                     op=ALU.add))(),
                 reads=[bk, "b_a"], writes=["zb"])
            if VARIANT == 1:
                P.op("dve", (lambda bank=bank: lambda e: e.tensor_copy(out=gkf[:], in_=bank[:, 256:512]))(),
                     reads=[bk], writes=["gkf"])
            else:
                P.op("act", (lambda bank=bank: lambda e: e.activation(out=gkf[:], in_=bank[:, 256:512], func=AF.Copy))(),
                     reads=[bk] + (["zb"] if VARIANT == 2 else []), writes=["gkf"])
            if BODY_LEVEL < 0:
                return
            P.op("act", lambda e: e.activation(out=lgh[:], in_=lgh[:], func=AF.Ln), reads=["lgh"], writes=["lgh"])
            P.op("act", lambda e: e.activation(out=zb[:], in_=zb[:], func=AF.Exp, scale=-1.0),
                 reads=["zb"], writes=["zb"])
            P.op("act", lambda e: e.activation(out=spg[:], in_=zb[:], func=AF.Ln, bias=1.0),
                 reads=["zb"], writes=["spg"])
            if BODY_LEVEL < 1:
                return
            P.op("pe", lambda e: e.matmul(pc2[:, 0:512], lhsT=L64, rhs=lgh[:], start=True, stop=True),
                 reads=["cst", "lgh"], writes=["pc2"])

            def fn(e):
                e.matmul(pc3[:, 0:256], lhsT=LGm, rhs=spg[:], start=True, stop=True)
                return e.matmul(pc3[:, 256:512], lhsT=UGm, rhs=spg[:], start=True, stop=True)
            P.op("pe", fn, reads=["cst", "spg"], writes=["pc3"])

            def fn(e):
                ins = None
                for h in range(4):
                    ins = e.matmul(psc[:, 2 * h:2 * h + 2], lhsT=lgh[:, 128 * h:128 * h + 128], rhs=ind[:, 0:2],
                                   start=True, stop=True)
                for j in range(2):
                    ins = e.matmul(psc[:, 8 + 2 * j:10 + 2 * j], lhsT=spg[:, 128 * j:128 * j + 128],
                                   rhs=ind[:, 2:4], start=True, stop=True)
                return ins
            P.op("pe", fn, reads=["cst", "lgh", "spg"], writes=["psc"])
            P.op("act", lambda e: e.activation(out=gdec[:], in_=psc[:, 0:12], func=AF.Exp),
                 reads=["psc"], writes=[("gdec" + sfx)])
            P.op("act", lambda e: e.activation(out=eb[:, 0:512], in_=pc2[:, 0:512], func=AF.Exp),
                 reads=["pc2"], writes=["eb_h"])
            P.op("act", lambda e: e.activation(out=enb[:, 0:512], in_=pc2[:, 0:512], func=AF.Exp, scale=-1.0),
                 reads=["pc2"], writes=["enb_h"])
            P.op("act", lambda e: e.activation(out=eb[:, 512:768], in_=pc3[:, 0:256], func=AF.Exp),
                 reads=["pc3"], writes=["eb_g"])
            P.op("act", lambda e: e.activation(out=enb[:, 512:768], in_=pc3[:, 0:256], func=AF.Exp, scale=-1.0),
                 reads=["pc3"], writes=["enb_g"])
            for c in range(2):
                P.op("act", (lambda c=c: lambda e: e.activation(out=erb[:, c, 512:768], in_=pc3[:, 256:512], func=AF.Exp,
                                                                bias=cbias[:, c:c + 1]))(),
                     reads=["pc3", "cst"], writes=["erb_g%d" % c])
            P.op("pe", lambda e: e.matmul(pc2[:, 0:512], lhsT=U64, rhs=lgh[:], start=True, stop=True),
                 reads=["cst", "lgh"], writes=["pc2"])
            for c in range(2):
                P.op("act", (lambda c=c: lambda e: e.activation(out=erb[:, c, 0:512], in_=pc2[:, 0:512], func=AF.Exp,
                                                                bias=cbias[:, c:c + 1]))(),
                     reads=["pc2", "cst"], writes=["erb_h%d" % c])
            if BODY_LEVEL < 2:
                return
            bank, bk = proj_group(2)
            P.op("dve", (lambda bank=bank: lambda e: e.scalar_tensor_tensor(
                out=qk[:, 0:512], in0=bank[:, 0:512], scalar=128.0 ** -0.5, in1=eb[:, 0:512],
                op0=ALU.mult, op1=ALU.mult))(), reads=[bk, "eb_h"], writes=["qk_qh"])
            bank, bk = proj_group(3)

            def fn(e, bank=bank):
                ins = None
                qv = qk[:, 512:1024].rearrange("p (j x) -> p j x", j=2)
                bv = bank[:, 0:256].rearrange("p (j r f) -> p j r f", j=2, r=2)
                ev = eb[:, 512:768].rearrange("p (j r f) -> p j r f", j=2, r=2)
                for r in range(2):
                    ins = e.scalar_tensor_tensor(out=qv[:, :, 192 * r:192 * r + 64], in0=bv[:, :, r, :],
                                                 scalar=64.0 ** -0.5, in1=ev[:, :, r, :], op0=ALU.mult, op1=ALU.mult)
                return ins
            P.op("dve", fn, reads=[bk, "eb_g", "qk_zero"], writes=["qk_qg"])
            P.op("pool", lambda e: e.tensor_tensor(out=qk[:, 1024:1536], in0=omf[:], in1=enb[:, 0:512], op=ALU.mult),
                 reads=["omf", "enb_h"], writes=["qk_kh"])
            P.op("pool", lambda e: e.tensor_tensor(out=qk[:, 1536:1792], in0=gkf[:], in1=enb[:, 512:768], op=ALU.mult),
                 reads=["gkf", "enb_g"], writes=["qk_kg"])
            for c in range(2):
                P.op("pool", (lambda c=c: lambda e: e.tensor_tensor(out=kh[:, c, 0:512], in0=omf[:], in1=erb[:, c, 0:512],
                                                                    op=ALU.mult))(),
                     reads=["omf", "erb_h%d" % c], writes=[("kh_h%d" % c + sfx)])
                P.op("pool", (lambda c=c: lambda e: e.tensor_tensor(out=kh[:, c, 512:768], in0=gkf[:],
                                                                    in1=erb[:, c, 512:768], op=ALU.mult))(),
                     reads=["gkf", "erb_g%d" % c], writes=[("kh_g%d" % c + sfx)])
            bank, bk = proj_group(4)
            P.op("act", (lambda bank=bank: lambda e: e.activation(out=vv[:, 0:512], in_=bank[:, 0:512], func=AF.Copy))(),
                 reads=[bk], writes=[("vv_h" + sfx)])
            bank, bk = proj_group(5)
            P.op("act", (lambda bank=bank: lambda e: e.activation(out=vv[:, 512:1024], in_=bank[:, 0:512], func=AF.Copy))(),
                 reads=[bk], writes=[("vv_g" + sfx)])
            bank, bk = proj_group(6)
            P.op("act", (lambda bank=bank: lambda e: e.activation(out=sil[:, 0:512], in_=bank[:, 0:512],
                                                                  func=AF.Silu))(), reads=[bk], writes=[("sil_h" + sfx), ("gg" + sfx)])
            bank, bk = proj_group(7)
            P.op("act", (lambda bank=bank: lambda e: e.activation(out=sil[:, 512:1024], in_=bank[:, 0:512],
                                                                  func=AF.Silu))(), reads=[bk], writes=[("sil_g" + sfx), ("gg" + sfx)])
            P.op("pool", lambda e: e.tensor_tensor(out=sil[:], in0=sil[:], in1=gain[:], op=ALU.mult),
                 reads=[("sil_h" + sfx), ("sil_g" + sfx), "gain"], writes=[("gg" + sfx), ("sil_h" + sfx), ("sil_g" + sfx)])
            if BODY_LEVEL < 3:
                return
            QK_KEYS = ["qk_qh", "qk_qg", "qk_kh", "qk_kg"]

            def fn(e):
                ins = None
                for j in range(8):
                    ins = e.transpose(ptr_bf[:, 128 * j:128 * j + 128], qk[:, 128 * j:128 * j + 128], ident_bf[:])
                return ins
            P.op("pe", fn, reads=QK_KEYS + ["ident_bf"], writes=["ptr"])
            P.op("dve", lambda e: e.tensor_copy(out=qkT[:, 0:8, :], in_=ptr_bf[:, 0:1024].rearrange("p (a b) -> p a b", b=128)),
                 reads=["ptr"], writes=[("qkT_a" + sfx)])

            def fn(e):
                ins = None
                for j in range(6):
                    ins = e.transpose(ptr_bf[:, 128 * j:128 * j + 128], qk[:, 1024 + 128 * j:1024 + 128 * j + 128],
                                      ident_bf[:])
                return ins
            P.op("pe", fn, reads=QK_KEYS + ["ident_bf"], writes=["ptr"])
            P.op("dve", lambda e: e.tensor_copy(out=qkT[:, 8:14, :], in_=ptr_bf[:, 0:768].rearrange("p (a b) -> p a b", b=128)),
                 reads=["ptr"], writes=[("qkT_b" + sfx)])
            QKT = [("qkT_a" + sfx), ("qkT_b" + sfx)]
            if BODY_LEVEL < 4:
                return

            def fn(e):
                ins = None
                for c in range(2):
                    cs = slice(64 * c, 64 * c + 64)
                    for h in range(4):
                        ins = e.matmul(psc[cs, 64 * h:64 * h + 64], lhsT=qkT[:, 8 + h, cs], rhs=qkT[:, h, cs],
                                       start=True, stop=True)
                    for g in range(4):
                        ins = e.matmul(psc[cs, 64 * (4 + g):64 * (4 + g) + 64], lhsT=qkT[:, 12 + g // 2, cs],
                                       rhs=qkT[:, 4 + g, cs], start=True, stop=True)
                return ins
            P.op("pe", fn, reads=QKT, writes=["psc"])

            def fn(e):
                ins = None
                for c in range(2):
                    cs = slice(64 * c, 64 * c + 64)
                    ins = e.tensor_tensor(out=ATf[cs, :, 64 * c:64 * c + 64],
                                          in0=psc[cs, 0:512].rearrange("p (h t) -> p h t", t=64),
                                          in1=maskrep[cs, :].rearrange("p (h t) -> p h t", t=64), op=ALU.mult)
                return ins
            P.op("dve", fn, reads=["psc", "cst", ("AT_zero" + sfx)], writes=[("AT" + sfx)])
            if BODY_LEVEL < 5:
                return
        def back(i):
            pr = i % 2
            sfx = "" if pr == 0 else "_B"
            qkT, ATf, vv, kh, gdec, sil, xt = QKT_BUF[pr], ATF_BUF[pr], VV_BUF[pr], KH_BUF[pr], GD_BUF[pr], SIL_BUF[pr], XT_BUF[pr]
            t0 = i * 128
            QKT = [("qkT_a" + sfx), ("qkT_b" + sfx)]
            for c in range(2):
                cs = slice(64 * c, 64 * c + 64)

                def fn(e, c=c, cs=cs):
                    ins = None
                    for bh in range(8):
                        po = PO[bh // 4]
                        col = 128 * (bh % 4)
                        e.matmul(po[cs, col:col + 128], lhsT=ATf[:, bh, cs], rhs=vv[:, 128 * bh:128 * bh + 128],
                                 start=True, stop=False)
                        sblk = bh if bh < 4 else 4 + (bh - 4) // 2
                        ins = e.matmul(po[cs, col:col + 128], lhsT=qkT[:, bh, cs], rhs=Sb[:, sblk, :],
                                       start=False, stop=True)
                    return ins
                P.op("pe", fn, reads=[("AT" + sfx), ("vv_h" + sfx), ("vv_g" + sfx), "Sb"] + QKT, writes=["po6_%d" % c, "po7_%d" % c])

                def fn(e, c=c):
                    ins = None
                    for h in range(4):
                        ins = e.matmul(pc2[:, 128 * h:128 * h + 128], lhsT=kh[:, c, 128 * h:128 * h + 128],
                                       rhs=vv[:, 128 * h:128 * h + 128], start=True, stop=True)
                    for g in range(4):
                        j, r = g // 2, g % 2
                        ins = e.matmul(pc3[64 * r:64 * r + 64, 128 * j:128 * j + 128],
                                       lhsT=kh[:, c, 512 + 64 * g:512 + 64 * g + 64],
                                       rhs=vv[:, 512 + 128 * g:512 + 128 * g + 128], start=True, stop=True)
                    return ins
                P.op("pe", fn, reads=[("kh_h%d" % c + sfx), ("kh_g%d" % c + sfx), ("vv_h" + sfx), ("vv_g" + sfx)], writes=["pc2", "pc3"])

                def fn(e, c=c):
                    ins = None
                    for h in range(4):
                        ins = e.scalar_tensor_tensor(out=Sf[:, h, :], in0=Sf[:, h, :],
                                                     scalar=gdec[:, 2 * h + c:2 * h + c + 1],
                                                     in1=pc2[:, 128 * h:128 * h + 128], op0=ALU.mult, op1=ALU.add)
                    for j in range(2):
                        ins = e.scalar_tensor_tensor(out=Sf[:, 4 + j, :], in0=Sf[:, 4 + j, :],
                                                     scalar=gdec[:, 8 + 2 * j + c:8 + 2 * j + c + 1],
                                                     in1=pc3[:, 128 * j:128 * j + 128], op0=ALU.mult, op1=ALU.add)
                    return ins
                P.op("dve", fn, reads=["Sf", ("gdec" + sfx), "pc2", "pc3"], writes=["Sf"])
                P.op("act", lambda e: e.activation(out=Sb[:], in_=Sf[:], func=AF.Copy), reads=["Sf"], writes=["Sb"])
            if BODY_LEVEL < 6:
                return
            for bh in range(8):
                pob = PO[bh // 4][:, 128 * (bh % 4):128 * (bh % 4) + 128]
                P.op("act", (lambda pob=pob, bh=bh: lambda e: e.activation(
                    out=sqj[:, bh, :], in_=pob, func=AF.Square, accum_out=ss[:, bh:bh + 1]))(),
                    reads=PO_KEYS, writes=["ss%d" % bh, "sqj%d" % bh])
            SS = ["ss%d" % b for b in range(8)]
            P.op("act", lambda e: e.activation(out=rstd[:], in_=ss[:], func=AF.Ln, scale=1.0 / 128.0, bias=EPS),
                 reads=SS, writes=["rstd"])
            P.op("act", lambda e: e.activation(out=rstd[:], in_=rstd[:], func=AF.Exp, scale=-0.5),
                 reads=["rstd"], writes=["rstd"])

            def fn(e):
                ins = None
                for bh in range(8):
                    pob = PO[bh // 4][:, 128 * (bh % 4):128 * (bh % 4) + 128]
                    ins = e.scalar_tensor_tensor(out=on[:, 128 * bh:128 * bh + 128], in0=pob,
                                                 scalar=rstd[:, bh:bh + 1], in1=sil[:, 128 * bh:128 * bh + 128],
                                                 op0=ALU.mult, op1=ALU.mult)
                return ins
            P.op("dve", fn, reads=PO_KEYS + ["rstd", ("gg" + sfx)], writes=["on"])

            def fn(e):
                ins = None
                for j in range(8):
                    ins = e.transpose(ptr_bf[:, 128 * j:128 * j + 128], on[:, 128 * j:128 * j + 128], ident_bf[:])
                return ins
            P.op("pe", fn, reads=["on", "ident_bf"], writes=["ptr"])
            P.op("dve", lambda e: e.tensor_copy(out=onT[:], in_=ptr_bf[:, 0:1024].rearrange("p (a b) -> p a b", b=128)), reads=["ptr"], writes=["onT"])
            if BODY_LEVEL < 7:
                return

            def fn(e):
                ins = None
                for n in range(2):
                    for j in range(8):
                        ins = e.matmul(PO[n][:, 0:512], lhsT=onT[:, j, :], rhs=w_out[:, j, 512 * n:512 * n + 512],
                                       start=(j == 0), stop=(j == 7))
                return ins
            P.op("pe", fn, reads=["onT"] + WOUT_KEYS, writes=PO_KEYS)

            def fn(e):
                ins = None
                for n in range(2):
                    ins = e.scalar_tensor_tensor(out=yy[:, 512 * n:512 * n + 512], in0=xt[:, 512 * n:512 * n + 512],
                                                 scalar=ALPHA, in1=PO[n][:, 0:512], op0=ALU.mult, op1=ALU.add)
                return ins
            P.op("dve", fn, reads=[("xt" + sfx)] + PO_KEYS, writes=["yy"])

            ln_tail(yy, "yy", yy, "yy", ln1g, ln1b, "ln1g", "ln1b")
            if stage == "A":
                P.dma("sp", (lambda t0=t0: lambda e: e.dma_start(out=out_d[t0:t0 + 128, :], in_=x1[:]))(),
                      reads=["yy"], writes=["out"], key="x1st")
                return
            P.dma("sp", (lambda t0=t0: lambda e: e.dma_start(out=X1_d[t0:t0 + 128, :], in_=x1[:]))(),
                  reads=["yy"], writes=["X1_%d" % i], key="x1st")
            P.op("pool", lambda e: e.tensor_copy(out=on[:], in_=yy[:]), reads=["yy"], writes=["on"])
            P.dma("sp", (lambda t0=t0: lambda e: e.dma_start(out=X1b_d[t0:t0 + 128, :], in_=on[:]))(),
                  reads=["on"], writes=["X1b_%d" % i], key="x1bst")
            for rnd in range(2):
                def fn(e, rnd=rnd):
                    ins = None
                    for j in range(4):
                        jj = 4 * rnd + j
                        ins = e.transpose(ptr[:, 128 * j:128 * j + 128], yy[:, 128 * jj:128 * jj + 128], ident)
                    return ins
                P.op("pe", fn, reads=["yy", "cst"], writes=["ptr"])
                P.op("dve", (lambda rnd=rnd: lambda e: e.tensor_copy(
                    out=x1T[:, 4 * rnd:4 * rnd + 4, :], in_=ptr[:, 0:512].rearrange("p (a b) -> p a b", b=128)))(),
                    reads=["ptr"], writes=["x1T_%d" % rnd])

            def fn(e):
                ins = None
                for k in range(8):
                    ins = e.matmul(psc[:, 0:72], lhsT=x1T[:, k, :], rhs=wr[:, k, :], start=(k == 0), stop=(k == 7))
                return ins
            P.op("pe", fn, reads=["x1T_0", "x1T_1", "wr"], writes=["psc"])
            P.op("act", (lambda i=i: lambda e: e.activation(out=LG[:, i, :], in_=psc[:, 0:72], func=AF.Copy))(),
                 reads=["psc"], writes=["LG_%d" % i])

        n_tiles = NTILES
        if n_tiles > 0:
            front(0)
        for i in range(n_tiles):
            if i + 1 < n_tiles:
                front(i + 1)
            back(i)

        if stage != "A":
            LGK = ["LG_%d" % i for i in range(NT)]
            NTL = n_tiles
            K_SIL = ["sil_h", "sil_g", "gg"]
            K_EB = ["eb_h", "eb_g"]
            K_ENB = ["enb_h", "enb_g"]
            K_QK = ["qk_qh", "qk_qg", "qk_kh", "qk_kg", "qk_zero"]
            tmp4 = sil[:].rearrange("p (i j g) -> p i j g", i=16, j=8)
            Rt = xt[:].rearrange("p (i e) -> p i e", i=16)
            E1 = lbl[:].rearrange("p a b -> p (a b)").rearrange("p (i e) -> p i e", e=64)
            E2 = gain[:].rearrange("p (i e) -> p i e", e=64)
            cum = xTs[:].rearrange("p a b -> p (a b)").rearrange("p (i e) -> p i e", e=64)
            Cb = on[:]
            sm = zb[:, 0:192].rearrange("p (a b) -> p a b", b=16)
            s8 = x1T[:].rearrange("p a b -> p (a b)").rearrange("p (n i j) -> p n i j", n=8, i=16)
            d0i = spg[:].bitcast(I32)[:, 0:16]
            d1i = spg[:].bitcast(I32)[:, 16:32]
            slb = gkf[:].bitcast(BF16)[:, 0:128]
            oneb = gkf[:].bitcast(BF16)[:, 128:256]
            V = lambda n: sm[:, n, :]
            gmax, gsum, pg, m1, m2, rr, den, gate1, gate2, d0f, d1f = [V(n) for n in range(11)]
            ohg, esub, esel, oh1, msk, oh2 = [s8[:, n, :, :] for n in range(6)]
            lgp = LG[:, :, 0:8]
            P.op("dve", lambda e: e.memset(lbl[:], 0.0), writes=["lbl0", "lbl1", "E1"])
            P.op("dve", lambda e: e.memset(gain[:], 0.0), writes=["gain", "E2"])
            P.op("dve", lambda e: e.memset(xTs[:], 0.0), writes=["xTs", "cum"])
            P.op("dve", lambda e: e.memset(zb[:], 0.0),
                 writes=["zb", "gmax", "gsum", "pg", "m1", "m2", "rr", "den", "gate1", "gate2", "d0f", "d1f"])
            P.op("dve", lambda e: e.memset(x1T[:], 0.0),
                 writes=["x1T_0", "x1T_1", "ohg", "esub", "esel", "oh1", "msk", "oh2"])
            P.op("dve", lambda e: e.memset(spg[:], 0.0), writes=["spg", "d0", "d1"])
            P.op("dve", lambda e: e.memset(gkf[:], 0.0), writes=["gkf", "slb", "oneb"])
            P.op("dve", lambda e: e.memset(wgaT[:], 0.0), writes=["wgaT", "y0"] + WGAT_B)
            P.op("dve", lambda e: e.tensor_copy(out=slb[:], in_=cst[:, C_SL:C_SL + 128]), reads=["cst"], writes=["slb"])
            P.op("dve", lambda e: e.tensor_copy(out=oneb[:], in_=cst[:, C_ONE:C_ONE + 128]), reads=["cst"], writes=["oneb"])

            def bc8(v):
                return v.unsqueeze(2).to_broadcast([128, 16, 8])
            P.op("dve", lambda e: e.tensor_reduce(out=gmax, in_=lgp, axis=AX.X, op=ALU.max), reads=LGK, writes=["gmax"])
            P.op("dve", lambda e: e.tensor_tensor(out=ohg, in0=lgp, in1=bc8(gmax), op=ALU.is_equal),
                 reads=LGK + ["gmax"], writes=["ohg"])
            P.op("dve", lambda e: e.tensor_tensor(out=esub, in0=lgp, in1=bc8(gmax), op=ALU.subtract),
                 reads=LGK + ["gmax"], writes=["esub"])
            P.op("act", lambda e: e.activation(out=esub, in_=esub, func=AF.Exp), reads=["esub"], writes=["esub"])
            P.op("dve", lambda e: e.tensor_reduce(out=gsum, in_=esub, axis=AX.X, op=ALU.add), reads=["esub"], writes=["gsum"])
            P.op("dve", lambda e: e.reciprocal(out=pg, in_=gsum), reads=["gsum"], writes=["pg"])
            P.op("dve", lambda e: e.tensor_tensor(
                out=tmp4, in0=LG[:, :, 8:72].rearrange("p i (g j) -> p i j g", g=8),
                in1=ohg.unsqueeze(2).to_broadcast([128, 16, 8, 8]), op=ALU.mult),
                reads=LGK + ["ohg"], writes=K_SIL)
            P.op("dve", lambda e: e.tensor_reduce(out=esel.rearrange("p i j -> p (i j)"),
                                                  in_=tmp4.rearrange("p i j g -> p (i j) g"), axis=AX.X, op=ALU.add),
                 reads=K_SIL, writes=["esel"])
            P.op("dve", lambda e: e.tensor_reduce(out=m1, in_=esel, axis=AX.X, op=ALU.max), reads=["esel"], writes=["m1"])
            P.op("dve", lambda e: e.tensor_tensor(out=oh1, in0=esel, in1=bc8(m1), op=ALU.is_equal),
                 reads=["esel", "m1"], writes=["oh1"])
            P.op("dve", lambda e: e.scalar_tensor_tensor(out=msk, in0=oh1, scalar=-1.0e30, in1=esel,
                                                         op0=ALU.mult, op1=ALU.add),
                 reads=["oh1", "esel"], writes=["msk"])
            P.op("dve", lambda e: e.tensor_reduce(out=m2, in_=msk, axis=AX.X, op=ALU.max), reads=["msk"], writes=["m2"])
            P.op("dve", lambda e: e.tensor_tensor(out=oh2, in0=msk, in1=bc8(m2), op=ALU.is_equal),
                 reads=["msk", "m2"], writes=["oh2"])
            P.op("dve", lambda e: e.tensor_tensor(out=rr, in0=m2, in1=m1, op=ALU.subtract), reads=["m1", "m2"], writes=["rr"])
            P.op("act", lambda e: e.activation(out=rr, in_=rr, func=AF.Exp), reads=["rr"], writes=["rr"])
            P.op("dve", lambda e: e.tensor_scalar(out=den, in0=rr, scalar1=1.0, scalar2=None, op0=ALU.add),
                 reads=["rr"], writes=["den"])
            P.op("dve", lambda e: e.reciprocal(out=den, in_=den), reads=["den"], writes=["den"])
            P.op("dve", lambda e: e.tensor_tensor(out=gate1, in0=pg, in1=den, op=ALU.mult), reads=["pg", "den"], writes=["gate1"])
            P.op("dve", lambda e: e.tensor_tensor(out=gate2, in0=gate1, in1=rr, op=ALU.mult), reads=["gate1", "rr"], writes=["gate2"])
            E1v = E1[:].rearrange("p i (g j) -> p i g j", g=8)
            E2v = E2[:].rearrange("p i (g j) -> p i g j", g=8)
            P.op("dve", lambda e: e.tensor_tensor(out=E1v, in0=ohg.unsqueeze(3).to_broadcast([128, 16, 8, 8]),
                                                  in1=oh1.unsqueeze(2).to_broadcast([128, 16, 8, 8]), op=ALU.mult),
                 reads=["ohg", "oh1"], writes=["E1"])
            P.op("dve", lambda e: e.tensor_tensor(out=E2v, in0=ohg.unsqueeze(3).to_broadcast([128, 16, 8, 8]),
                                                  in1=oh2.unsqueeze(2).to_broadcast([128, 16, 8, 8]), op=ALU.mult),
                 reads=["ohg", "oh2"], writes=["E2"])
            P.op("dve", lambda e: e.tensor_tensor(out=Cb, in0=E1[:].rearrange("p i e -> p (i e)"),
                                                  in1=E2[:].rearrange("p i e -> p (i e)"), op=ALU.add),
                 reads=["E1", "E2"], writes=["on"])
            RB = [PB[0], PB[1]]
            TB = [pc2, pc3]
            for hh in range(2):
                P.op("pe", (lambda hh=hh: lambda e: e.matmul(RB[hh][:, 0:512], lhsT=slb[:], rhs=Cb[:, 512 * hh:512 * hh + 512],
                                                             start=True, stop=True))(),
                     reads=["slb", "on"], writes=["pb%d" % hh])
                P.op("pe", (lambda hh=hh: lambda e: e.matmul(TB[hh][:, 0:512], lhsT=oneb[:], rhs=Cb[:, 512 * hh:512 * hh + 512],
                                                             start=True, stop=True))(),
                     reads=["oneb", "on"], writes=["pc%d" % (2 + hh)])
            P.op("dve", lambda e: e.memset(cum[:, 0, :], 0.0), writes=["cum"])
            for i in range(1, 16):
                src = TB[(i - 1) // 8][:, 64 * ((i - 1) % 8):64 * ((i - 1) % 8) + 64]
                P.op("dve", (lambda i=i, src=src: lambda e: e.tensor_tensor(out=cum[:, i, :], in0=cum[:, i - 1, :], in1=src,
                                                                            op=ALU.add))(),
                     reads=["cum", "pc2", "pc3"], writes=["cum"])
            for hh in range(2):
                P.op("dve", (lambda hh=hh: lambda e: e.tensor_tensor(
                    out=Rt[:, 8 * hh:8 * hh + 8, :], in0=RB[hh][:, 0:512].rearrange("p (i e) -> p i e", e=64),
                    in1=cum[:, 8 * hh:8 * hh + 8, :], op=ALU.add))(),
                    reads=["pb%d" % hh, "cum"], writes=["xt"])
            P.op("dve", lambda e: e.tensor_scalar(out=cum[:], in0=Rt, scalar1=float(CAP), scalar2=1.0e6,
                                                  op0=ALU.is_ge, op1=ALU.mult), reads=["xt"], writes=["cum"])
            P.op("dve", lambda e: e.tensor_tensor(out=Rt, in0=Rt, in1=cum[:], op=ALU.add), reads=["xt", "cum"], writes=["xt"])
            P.op("dve", lambda e: e.tensor_tensor(out=Rt, in0=Rt,
                                                  in1=cst[:, C_EB:C_EB + 64].unsqueeze(1).to_broadcast([128, 16, 64]),
                                                  op=ALU.add), reads=["xt", "cst"], writes=["xt"])
            for kk, (Ek, ekey, df, di, dkey) in enumerate([(E1, "E1", d0f, d0i, "d0"), (E2, "E2", d1f, d1i, "d1")]):
                P.op("dve", (lambda Ek=Ek: lambda e: e.tensor_tensor(out=Ek[:], in0=Ek[:], in1=Rt, op=ALU.mult))(),
                     reads=[ekey, "xt"], writes=[ekey])
                P.op("dve", (lambda Ek=Ek, df=df: lambda e: e.tensor_reduce(out=df, in_=Ek[:], axis=AX.X, op=ALU.add))(),
                     reads=[ekey], writes=[dkey + "f"])
                P.op("dve", (lambda df=df, di=di: lambda e: e.tensor_copy(out=di[:], in_=df))(),
                     reads=[dkey + "f"], writes=[dkey])

            x1b = vv
            K_VV = ["vv_h", "vv_g"]
            for n8 in range(8):
                P.dma("sp", (lambda n8=n8: lambda e: e.dma_start(
                    out=xb_d[1024 * n8:1024 * n8 + 1024, :], in_=zeros_d))(),
                    writes=["xbz%d" % n8], key="xbz")
            XBZ = ["xbz%d" % n for n in range(8)]
            c4 = [vv, VV_BUF[1], on, ATF_BUF[0][:].rearrange("p a b -> p (a b)")]
            c4k = [K_VV, ["vv_h_B", "vv_g_B"], ["on"], ["AT", "AT_zero"]]
            for i in range(NTL):
                t0 = i * 128
                cb, ck = c4[i % 4], c4k[i % 4]
                P.dma("sp", (lambda t0=t0, cb=cb: lambda e: e.dma_start(out=cb[:], in_=X1b_d[t0:t0 + 128, :]))(),
                      reads=["X1b_%d" % i], writes=ck, key="c4ld%d" % (i % 4))
                for kk, (di, dkey) in enumerate([(d0i, "d0"), (d1i, "d1")]):
                    P.dma("pool", (lambda i=i, di=di, cb=cb: lambda e: e.indirect_dma_start(
                        out=xb_d[:, :], out_offset=bass.IndirectOffsetOnAxis(ap=di[:, i:i + 1], axis=0),
                        in_=cb[:], in_offset=None, bounds_check=breg(e), oob_is_err=False))(),
                        reads=ck + [dkey] + XBZ, writes=["xbs_%d_%d" % (i, kk)], key="scat%d" % (i % 4))
            XBS = ["xbs_%d_%d" % (i, kk) for i in range(NTL) for kk in range(2)]

            wflat = w_all[:].rearrange("p k c -> p (k c)")
            woflat = w_out[:].rearrange("p k c -> p (k c)")
            wgu = [wflat[:, 8192 * b:8192 * b + 8192].rearrange("p (a n) -> p a n", n=512) for b in range(3)]
            wdn = [wflat[:, 24576:28672].rearrange("p (a n) -> p a n", n=1024),
                   woflat[:, 0:4096].rearrange("p (a n) -> p a n", n=1024),
                   woflat[:, 4096:8192].rearrange("p (a n) -> p a n", n=1024)]
            s0b = stg[0][:].bitcast(BF16)
            s1b = stg[1][:].bitcast(BF16)
            xblk = [s0b[:, 0:1024], s0b[:, 1024:2048], s1b[:, 1024:2048], s1b[:, 2048:3072]]
            xbT = [s0b[:, 2048:3072].rearrange("p (a b) -> p a b", b=128), s0b[:, 3072:4096].rearrange("p (a b) -> p a b", b=128)]
            hT = [s1b[:, 0:512].rearrange("p (a b) -> p a b", b=128), s1b[:, 512:1024].rearrange("p (a b) -> p a b", b=128)]
            sgt = [eb[:, 0:512], enb[:, 0:512]]
            K_SG = [K_EB, K_ENB]
            yblk = [sil, yy]
            K_YB = [K_SIL, ["yy"]]
            GU = [(PB[0], PB[1], "pb0", "pb1"), (pc2, pc3, "pc2", "pc3")]
            NEXP = NE

            def wload(ex):
                b = ex % 3
                ex_gu = WALL_KEYS if ex < 3 else []
                ex_dn = (WALL_KEYS if b == 0 else WOUT_KEYS) if ex < 3 else []
                P.dma("pool", (lambda ex=ex, b=b: lambda e: e.dma_start(
                    out=wgu[b].rearrange("p a n -> p (a n)").rearrange("p (c m) -> p c m", m=1024),
                    in_=wgu_d[ex].rearrange("p (c m) -> p c m", m=1024)))(),
                    writes=["wgu%d" % b] + ex_gu, key="wgu%d" % b)
                P.dma("pool", (lambda ex=ex, b=b: lambda e: e.dma_start(
                    out=wdn[b].rearrange("p a n -> p (a n)").rearrange("p (c m) -> p c m", m=1024),
                    in_=wdn_d[ex].rearrange("p (c m) -> p c m", m=1024)))(),
                    writes=["wdn%d" % b] + ex_dn, key="wdn%d" % b)
            P.op("dve", lambda e: e.memset(stg[0][:], 0.0), writes=["stg0", "xblk0", "xblk1", "xbT0", "xbT1"] + STG0_B)
            P.op("dve", lambda e: e.memset(stg[1][:], 0.0), writes=["stg1", "hT0", "hT1", "xblk2", "xblk3"] + STG1_B)
            def xload(ex):
                q4 = ex % 4
                P.dma("sp", (lambda ex=ex, q4=q4: lambda e: e.dma_start(out=xblk[q4], in_=xb_d[ex * 128:(ex + 1) * 128, :]))(),
                      reads=XBS + XBZ, writes=["xblk%d" % q4], key="xblk%d" % q4)

            def stage1(ex):
                b = ex % 3
                q = ex % 2
                q4 = ex % 4

                def fn(e, q4=q4):
                    ins = None
                    for j in range(8):
                        ins = e.transpose(ptr_bf[:, 128 * j:128 * j + 128], xblk[q4][:, 128 * j:128 * j + 128], ident_bf[:])
                    return ins
                P.op("pe", fn, reads=["xblk%d" % q4, "ident_bf"], writes=["ptr"])
                P.op("dve", (lambda q=q: lambda e: e.tensor_copy(out=xbT[q], in_=ptr_bf[:, 0:1024].rearrange("p (a b) -> p a b", b=128)))(),
                     reads=["ptr"], writes=["xbT%d" % q])
                gb, ub, gk_, uk_ = GU[q]
                for m, (bank, bkey) in enumerate([(gb, gk_), (ub, uk_)]):
                    def fn(e, m=m, b=b, q=q, bank=bank):
                        ins = None
                        for nchunk in range(4):
                            for k in range(8):
                                ins = e.matmul(bank[:, 128 * nchunk:128 * nchunk + 128],
                                               lhsT=wgu[b][:, 8 * m + k, 128 * nchunk:128 * nchunk + 128],
                                               rhs=xbT[q][:, k, :], start=(k == 0), stop=(k == 7))
                        return ins
                    P.op("pe", fn, reads=["xbT%d" % q, "wgu%d" % b], writes=[bkey])
                P.op("act", (lambda q=q, gb=gb: lambda e: e.activation(out=sgt[q], in_=gb[:, 0:512], func=AF.Silu))(),
                     reads=[gk_], writes=K_SG[q])
                P.op("dve", (lambda q=q, ub=ub: lambda e: e.tensor_tensor(
                    out=hT[q].rearrange("p a b -> p (a b)"), in0=sgt[q], in1=ub[:, 0:512], op=ALU.mult))(),
                    reads=K_SG[q] + [uk_], writes=["hT%d" % q])

            def stage2(ex):
                b = ex % 3
                q = ex % 2

                def fn(e, b=b, q=q):
                    ins = None
                    for half in range(2):
                        for nchunk in range(4):
                            ins = e.matmul(PO[half][:, 0:512], lhsT=hT[q][:, nchunk, :],
                                           rhs=wdn[b][:, nchunk, 512 * half:512 * half + 512],
                                           start=(nchunk == 0), stop=(nchunk == 3))
                    return ins
                P.op("pe", fn, reads=["hT%d" % q, "wdn%d" % b], writes=PO_KEYS)
                P.op("act", (lambda q=q: lambda e: e.activation(out=yblk[q][:, 0:512], in_=po6[:, 0:512], func=AF.Copy))(),
                     reads=PO_KEYS, writes=K_YB[q])
                P.op("dve", (lambda q=q: lambda e: e.tensor_copy(out=yblk[q][:, 512:1024], in_=po7[:, 0:512]))(),
                     reads=PO_KEYS, writes=K_YB[q])
                P.dma("act", (lambda ex=ex, q=q: lambda e: e.dma_start(out=yb_d[ex * 128:(ex + 1) * 128, :], in_=yblk[q][:]))(),
                      reads=K_YB[q], writes=["yb_%d" % ex], key="ybst%d" % q)

            for ex0 in range(3):
                xload(ex0)
            wload(0)
            wload(1)
            stage1(0)
            for ex in range(NEXP):
                if ex + 2 < NEXP:
                    wload(ex + 2)
                if ex + 3 < NEXP:
                    xload(ex + 3)
                if ex + 1 < NEXP:
                    stage1(ex + 1)
                stage2(ex)
            YBK = ["yb_%d" % ex for ex in range(NEXP)]

            P.dma("sp", lambda e: e.dma_start(out=ln1g[:], in_=ln2g_d[0:1, :].to_broadcast([128, D])),
                  writes=["ln1g"], key="ln1g")
            P.dma("sp", lambda e: e.dma_start(out=ln1b[:], in_=ln2b_d[0:1, :].to_broadcast([128, D])),
                  writes=["ln1b"], key="ln1b")
            wf32 = wflat.bitcast(F32)
            WD_KEYS = ["wgu0", "wgu1", "wgu2", "wdn0"]
            EK = ["e_y0_0", "e_y0_1", "e_y1_0", "e_y1_1", "e_x_0", "e_x_1", "e_a_0", "e_a_1"]
            P.op("dve", lambda e: e.memset(wf32[:, 0:8192], 0.0), writes=WD_KEYS + EK)
            EY0 = [wf32[:, 0:1024], wf32[:, 1024:2048]]
            EY1 = [wf32[:, 2048:3072], wf32[:, 3072:4096]]
            EX = [wf32[:, 4096:5120], wf32[:, 5120:6144]]
            EA = [wf32[:, 6144:7168], wf32[:, 7168:8192]]
            for i in range(NTL):
                t0 = i * 128
                q = i % 2
                y0, y1, xq, aq = EY0[q], EY1[q], EX[q], EA[q]
                ky0, ky1, kx, ka = "e_y0_%d" % q, "e_y1_%d" % q, "e_x_%d" % q, "e_a_%d" % q
                P.dma("sp", (lambda t0=t0, xq=xq: lambda e: e.dma_start(out=xq, in_=X1_d[t0:t0 + 128, :]))(),
                      reads=["X1_%d" % i], writes=[kx], key=kx)
                P.dma("pool", (lambda i=i, y0=y0: lambda e: e.indirect_dma_start(
                    out=y0, out_offset=None, in_=yb_d[:, :],
                    in_offset=bass.IndirectOffsetOnAxis(ap=d0i[:, i:i + 1], axis=0),
                    bounds_check=breg(e), oob_is_err=False))(),
                    reads=YBK + ["d0"], writes=[ky0], key="g" + ky0)
                P.dma("pool", (lambda i=i, y1=y1: lambda e: e.indirect_dma_start(
                    out=y1, out_offset=None, in_=yb_d[:, :],
                    in_offset=bass.IndirectOffsetOnAxis(ap=d1i[:, i:i + 1], axis=0),
                    bounds_check=breg(e), oob_is_err=False))(),
                    reads=YBK + ["d1"], writes=[ky1], key="g" + ky1)
                P.op("act", (lambda xq=xq: lambda e: e.activation(out=xq, in_=xq, func=AF.Copy, scale=ALPHA))(),
                     reads=[kx], writes=[kx])
                P.op("dve", (lambda i=i, y0=y0, xq=xq, aq=aq: lambda e: e.scalar_tensor_tensor(
                    out=aq, in0=y0, scalar=gate1[:, i:i + 1], in1=xq, op0=ALU.mult, op1=ALU.add))(),
                    reads=[ky0, "gate1", kx], writes=[ka])
                P.op("dve", (lambda i=i, y1=y1, aq=aq: lambda e: e.scalar_tensor_tensor(
                    out=aq, in0=y1, scalar=gate2[:, i:i + 1], in1=aq, op0=ALU.mult, op1=ALU.add))(),
                    reads=[ky1, "gate2", ka], writes=[ka])
                ln_tail(aq, ka, aq, ka, ln1g, ln1b, "ln1g", "ln1b")
                P.dma("sp", (lambda t0=t0, aq=aq: lambda e: e.dma_start(out=out_d[t0:t0 + 128, :], in_=aq))(),
                      reads=[ka], writes=["out"], key="outst%d" % q)

        final_keys = ["out"]
        P.analyze(final_wait_keys=final_keys, do_schedule=SCHEDULE)
        print("n_ops", len(P.ops), "n_waits", P.n_waits, "sems", len(P.sem_names), "sched_us", getattr(P, "sched_len", None))
        P.emit(nc, st)
    return nc


def prep_inputs(inputs):
    x = np.asarray(inputs["x"], np.float32)
    w_in = np.asarray(inputs["w_in"], np.float32)[0]
    sl = lambda a, b: w_in[:, a:b]
    w_in_r = np.ascontiguousarray(np.concatenate(
        [sl(512, 1024), sl(2304, 2560), sl(0, 512), sl(2048, 2304), sl(1024, 1536), sl(2560, 3072),
         sl(1536, 2048), sl(3088, 3600)], axis=1))
    w_gaT = np.ascontiguousarray(w_in[:, 3072:3088].T)
    shared = {
        "w_in_r": w_in_r,
        "w_gaT": w_gaT,
        "w_a2": np.ascontiguousarray(inputs["w_a2"][0]),
        "b_a": np.ascontiguousarray(inputs["b_a"][0:1]),
        "lb_logits": np.ascontiguousarray(inputs["lb_logits"]),
        "norm_hg": np.ascontiguousarray(np.concatenate([inputs["norm_h"][0], inputs["norm_g"][0]])[None, :]),
        "w_out": np.ascontiguousarray(inputs["w_out"][0]),
        "ln1_g": np.ascontiguousarray(inputs["ln1_g"][0:1]),
        "ln1_b": np.ascontiguousarray(inputs["ln1_b"][0:1]),
        "w_router": np.ascontiguousarray(np.concatenate([inputs["w_group_router"][0],
                                                         inputs["w_expert_router"][0]], axis=1)),
        "w_gu": np.ascontiguousarray(np.concatenate(
            [np.asarray(inputs["w_gate"][0]).reshape(NE, 8, 128, 512).transpose(0, 2, 1, 3),
             np.asarray(inputs["w_up"][0]).reshape(NE, 8, 128, 512).transpose(0, 2, 1, 3)], axis=2).reshape(NE, 128, 8192)),
        "w_dn": np.ascontiguousarray(np.asarray(inputs["w_down"][0]).reshape(NE, 4, 128, D).transpose(0, 2, 1, 3)
                                     .reshape(NE, 128, 4096)),
        "ln2_g": np.ascontiguousarray(inputs["ln2_g"][0:1]),
        "ln2_b": np.ascontiguousarray(inputs["ln2_b"][0:1]),
        "cst": make_consts(),
    }
    shared = {k: np.asarray(v, np.float32) for k, v in shared.items()}
    import ml_dtypes
    shared["zeros_bf"] = np.zeros((1024, D), ml_dtypes.bfloat16)
    in_maps = []
    for c in range(NCORES):
        m = dict(shared)
        m["x"] = np.ascontiguousarray(x[c])
        m["xT"] = np.ascontiguousarray(x[c].reshape(NT, 128, 8, 128).transpose(0, 3, 2, 1).reshape(NT, 128, 1024))
        in_maps.append(m)
    return in_maps


STAGE = "full"
SCHEDULE = True
VARIANT = 2
BODY_LEVEL = 9
SETUP_LEVEL = 9
NTILES = NT


def kernel(**inputs):
    in_maps = prep_inputs(inputs)
    nc = build_program(STAGE)
    if STAGE == "A":
        for m in in_maps:
            for k in ("w_gu", "w_dn"):
                m.pop(k)
    res = run_bass_kernel_spmd(nc, in_maps, core_ids=list(range(NCORES)))
    out = np.stack([np.asarray(res.results[c]["out"], np.float32) for c in range(NCORES)], axis=0)
    return out
```

```python
import numpy as np
from contextlib import ExitStack
import concourse.bass as bass
import concourse.mybir as mybir
from concourse.bass_utils import run_bass_kernel_spmd

F32 = mybir.dt.float32
BF16 = mybir.dt.bfloat16
I32 = mybir.dt.int32
AF = mybir.ActivationFunctionType
ALU = mybir.AluOpType
AX = mybir.AxisListType

NCORES = 8
T = 2048
D = 1024
NT = T // 128
ALPHA = 2.0 ** 0.25
EPS = 1e-5
NE = 64
CAP = 128
NSLOT = NE * CAP
WCOLS = 3840

ENG_NAMES = ("pe", "act", "dve", "pool", "sp")


class Op:
    __slots__ = ("eng", "fn", "reads", "writes", "dma_key", "idx", "deps", "signal",
                 "sem", "val", "waits", "dma_bytes")

    def __init__(self, eng, fn, reads, writes, dma_key):
        self.eng = eng
        self.fn = fn
        self.reads = tuple(reads)
        self.writes = tuple(writes)
        self.dma_key = dma_key
        self.deps = set()
        self.signal = False
        self.sem = None
        self.val = 0
        self.waits = []


_EST = [False]


class _Ins:
    def then_inc(self, *a, **k):
        return self


class _Rec:
    def __init__(self):
        self.calls = []

    def __getattr__(self, name):
        def f(*a, **k):
            self.calls.append((name, a, k))
            return _Ins()
        return f


def _fsize(ap):
    n = 1
    for d in list(ap.shape)[1:]:
        n *= int(d)
    return n


def _dsz(ap):
    return int(mybir.dt.size(ap.dtype))


class Prog:
    def __init__(self):
        self.ops = []

    def op(self, eng, fn, reads=(), writes=()):
        o = Op(eng, fn, reads, writes, None)
        o.idx = len(self.ops)
        self.ops.append(o)
        return o

    def dma(self, queue, fn, reads=(), writes=(), key=None):
        o = Op(queue, fn, reads, writes, key)
        o.idx = len(self.ops)
        self.ops.append(o)
        return o

    def analyze(self, final_wait_keys=(), do_schedule=False):
        writers, readers = {}, {}
        for o in self.ops:
            deps = set()
            for b in o.reads:
                deps.update(writers.get(b, ()))
            for b in o.writes:
                deps.update(writers.get(b, ()))
                deps.update(readers.get(b, ()))
            deps.discard(o.idx)
            o.deps = deps
            for b in o.reads:
                readers.setdefault(b, []).append(o.idx)
            for b in o.writes:
                writers[b] = [o.idx]
                readers[b] = []
        self.final_deps = set()
        for b in final_wait_keys:
            self.final_deps.update(writers.get(b, ()))
        if do_schedule:
            self.schedule()
        for o in self.ops:
            for d in o.deps:
                self.ops[d].signal = True
        for d in self.final_deps:
            self.ops[d].signal = True
        cnt = {}
        for o in self.ops:
            if o.dma_key is not None:
                s = ("dma", o.dma_key)
                o.signal = True
                cnt[s] = cnt.get(s, 0) + 16
                o.sem, o.val = s, cnt[s]
            elif o.signal:
                s = ("eng", o.eng)
                cnt[s] = cnt.get(s, 0) + 1
                o.sem, o.val = s, cnt[s]
        self.sem_names = sorted(cnt.keys(), key=str)
        seen = {e: {} for e in ENG_NAMES}
        know = {}
        nw = 0
        for o in self.ops:
            sn = seen[o.eng]
            need = {}
            for d in o.deps:
                do = self.ops[d]
                if sn.get(do.sem, 0) >= do.val:
                    continue
                if need.get(do.sem, 0) < do.val:
                    need[do.sem] = do.val
            for d in o.deps:
                do = self.ops[d]
                if do.sem in need:
                    for s, v in know[d].items():
                        if sn.get(s, 0) < v:
                            sn[s] = v
            o.waits = sorted(need.items(), key=str)
            nw += len(o.waits)
            if o.signal:
                k = dict(sn)
                k[o.sem] = max(k.get(o.sem, 0), o.val)
                know[o.idx] = k
        self.final_waits = {}
        for d in self.final_deps:
            do = self.ops[d]
            if self.final_waits.get(do.sem, 0) < do.val:
                self.final_waits[do.sem] = do.val
        for s, v in cnt.items():
            if s[0] == "dma" and self.final_waits.get(s, 0) < v:
                self.final_waits[s] = v
        self.n_waits = nw
        self.max_cnt = cnt


    def estimate(self, o):
        rec = _Rec()
        _EST[0] = True
        try:
            o.fn(rec)
        finally:
            _EST[0] = False
        eng_t, lat = 0.0, 0.0
        for name, a, k in rec.calls:
            def arg(nm, pos):
                return k[nm] if nm in k else (a[pos] if len(a) > pos else None)
            if name == "matmul":
                rhs = arg("rhs", 2)
                n = _fsize(rhs)
                f = 4.0 if _dsz(rhs) == 4 else 1.0
                eng_t += 0.06 + f * max(n, 64) * 0.00042
            elif name == "transpose":
                eng_t += 0.12
            elif name in ("dma_start", "indirect_dma_start"):
                out = arg("out", 0)
                inn = k.get("in_", None)
                byts = _fsize(out) * out.shape[0] * _dsz(out)
                if inn is not None and hasattr(inn, "shape"):
                    byts = min(byts, _fsize(inn) * inn.shape[0] * _dsz(inn))
                eng_t += 0.8 if o.eng == "pool" else 0.12
                lat = max(lat, 2.2 + byts / (90e3 if name == "indirect_dma_start" else 300e3))
                o.dma_bytes = byts
            elif name == "to_reg":
                pass
            else:
                src = None
                for cnd in [k.get("in_"), k.get("in0"), k.get("data"), k.get("out")] + list(a):
                    if cnd is not None and hasattr(cnd, "shape"):
                        src = cnd
                        break
                n = _fsize(src) if src is not None else 64
                if o.eng == "act":
                    eng_t += 0.22 + n / 900.0
                elif o.eng == "pool":
                    eng_t += 0.35 + n / 330.0
                else:
                    eng_t += 0.12 + n / 800.0
        return eng_t, lat

    def schedule(self):
        n = len(self.ops)
        cost, lat = [0.0] * n, [0.0] * n
        for o in self.ops:
            o.dma_bytes = 0
            c, l = self.estimate(o)
            cost[o.idx], lat[o.idx] = c, l
        users = [[] for _ in range(n)]
        for o in self.ops:
            for d in o.deps:
                users[d].append(o.idx)
        prio = [0.0] * n
        for i in range(n - 1, -1, -1):
            m = 0.0
            for u in users[i]:
                if prio[u] > m:
                    m = prio[u]
            prio[i] = cost[i] + lat[i] + m
        ndep = [len(o.deps) for o in self.ops]
        est = [0.0] * n
        fin = [0.0] * n
        start = [0.0] * n
        t_eng = {e: 0.0 for e in ENG_NAMES}
        cand = {e: [] for e in ENG_NAMES}
        for o in self.ops:
            if ndep[o.idx] == 0:
                cand[o.eng].append(o.idx)
        dma_free = 0.0
        done = 0
        WINDOW = 0.4
        while done < n:
            best = None
            for e in ENG_NAMES:
                cl = cand[e]
                if not cl:
                    continue
                te = t_eng[e]
                ms = min(max(te, est[i]) for i in cl)
                pick = None
                for i in cl:
                    st_ = max(te, est[i])
                    if st_ <= ms + WINDOW:
                        key = (-prio[i], i)
                        if pick is None or key < pick[0]:
                            pick = (key, i, st_)
                if best is None or pick[2] < best[2]:
                    best = (e, pick[1], pick[2])
            e, i, st_ = best
            cand[e].remove(i)
            start[i] = st_
            t_eng[e] = st_ + cost[i]
            if lat[i] > 0:
                byts = self.ops[i].dma_bytes
                if byts >= (1 << 20):
                    b0 = max(st_ + cost[i], dma_free)
                    fin[i] = b0 + lat[i]
                    dma_free = fin[i] - 2.2
                else:
                    fin[i] = st_ + cost[i] + lat[i]
            else:
                fin[i] = st_ + cost[i]
            done += 1
            for u in users[i]:
                if fin[i] > est[u]:
                    est[u] = fin[i]
                ndep[u] -= 1
                if ndep[u] == 0:
                    cand[self.ops[u].eng].append(u)
        order = sorted(range(n), key=lambda i: (start[i], i))
        remap = {old: new for new, old in enumerate(order)}
        new_ops = [self.ops[i] for i in order]
        for o in new_ops:
            o.deps = set(remap[d] for d in o.deps)
        for new, o in enumerate(new_ops):
            o.idx = new
        self.final_deps = set(remap[d] for d in self.final_deps)
        self.ops = new_ops
        self.sched_len = max(fin) if n else 0.0

    def emit(self, nc, stack):
        sems = {}
        for i, s in enumerate(self.sem_names):
            sems[s] = stack.enter_context(nc.semaphore("s%d" % i))
        block = stack.enter_context(nc.Block())
        per = {e: [o for o in self.ops if o.eng == e] for e in ENG_NAMES}
        final_waits = self.final_waits

        def run(e, ename):
            for o in per[ename]:
                for s, v in o.waits:
                    e.wait_ge(sems[s], v)
                ins = o.fn(e)
                if o.signal:
                    ins.then_inc(sems[o.sem], 16 if o.dma_key is not None else 1)
            if ename == "sp":
                for s, v in sorted(final_waits.items(), key=str):
                    e.wait_ge(sems[s], v)

        @block.tensor
        def _(e):
            run(e, "pe")

        @block.scalar
        def _(e):
            run(e, "act")

        @block.vector
        def _(e):
            run(e, "dve")

        @block.gpsimd
        def _(e):
            run(e, "pool")

        @block.sync
        def _(e):
            run(e, "sp")


C_ID, C_L, C_U, C_LG, C_UG, C_SL, C_ONE, C_MASK, C_IND, C_EB = 0, 128, 256, 384, 512, 640, 768, 896, 1408, 1412
C_CB = 1476
C_TOT = 1478


def make_consts():
    c = np.zeros((128, C_TOT), np.float32)
    s = np.arange(128)[:, None]
    t = np.arange(128)[None, :]
    same = (s // 64) == (t // 64)
    c[:, C_ID:C_ID + 128] = np.eye(128)
    L = ((s <= t) & same).astype(np.float32)
    U = ((s > t) & same).astype(np.float32)
    c[:, C_L:C_L + 128] = L
    c[:, C_U:C_U + 128] = U
    c[:, C_LG:C_LG + 128] = -L / 16.0
    c[:, C_UG:C_UG + 128] = -U / 16.0
    c[:, C_SL:C_SL + 128] = (s < t).astype(np.float32)
    c[:, C_ONE:C_ONE + 128] = 1.0
    tt = np.arange(64)[None, :]
    m = ((s % 64) <= tt).astype(np.float32)
    c[:, C_MASK:C_MASK + 512] = np.tile(m, (1, 8))
    ind = (np.arange(128)[:, None] // 64 == np.arange(2)[None, :]).astype(np.float32)
    c[:, C_IND:C_IND + 2] = ind
    c[:, C_IND + 2:C_IND + 4] = -ind / 16.0
    c[:, C_EB:C_EB + 64] = (np.arange(64) * CAP)[None, :]
    c[:, C_CB] = np.where(np.arange(128) < 64, 0.0, -30000.0)
    c[:, C_CB + 1] = np.where(np.arange(128) >= 64, 0.0, -30000.0)
    return c


def build_program(stage="full"):
    nc = bass.Bass("TRN2", target_bir_lowering=False)
    P = Prog()

    def din(name, shape, dt=F32):
        return nc.dram_tensor(name, list(shape), dt, kind="ExternalInput").ap()

    x_d = din("x", [T, D])
    xT_d = din("xT", [NT, 128, 1024])
    w_in_d = din("w_in_r", [D, 3584])
    w_gaT_d = din("w_gaT", [16, D])
    w_a2_d = din("w_a2", [16, 256])
    b_a_d = din("b_a", [1, 256])
    lbl_d = din("lb_logits", [2, 512])
    gain_d = din("norm_hg", [1, 1024])
    w_out_d = din("w_out", [D, D])
    ln1g_d = din("ln1_g", [1, D])
    ln1b_d = din("ln1_b", [1, D])
    wr_d = din("w_router", [D, 72])
    if stage != "A":
        wgu_d = din("w_gu", [NE, 128, 8192])
        wdn_d = din("w_dn", [NE, 128, 4096])
    ln2g_d = din("ln2_g", [1, D])
    ln2b_d = din("ln2_b", [1, D])
    cst_d = din("cst", [128, C_TOT])
    zeros_d = din("zeros_bf", [1024, D], BF16)
    out_d = nc.dram_tensor("out", [T, D], F32, kind="ExternalOutput").ap()
    X1_d = nc.dram_tensor("X1s", [T, D], F32, kind="Internal").ap()
    xb_d = nc.dram_tensor("xbs", [NSLOT, D], BF16, kind="Internal").ap()
    X1b_d = nc.dram_tensor("X1bs", [T, D], BF16, kind="Internal").ap()
    yb_d = nc.dram_tensor("ybs", [NSLOT, D], F32, kind="Internal").ap()

    _breg = {}

    def breg(e):
        if _EST[0]:
            return 0
        if "r" not in _breg:
            _breg["r"] = e.to_reg(NSLOT - 1)
        return _breg["r"]

    st = ExitStack()
    with st:
        def sb(name, shape, dt=F32):
            return st.enter_context(nc.sbuf_tensor("sb_" + name, list(shape), dt))

        def ps(name):
            return st.enter_context(nc.psum_tensor(name, [128, 512], F32))

        PB = [ps("pb0"), ps("pb1")]
        pc2, pc3, ptr, psc, po6, po7 = ps("pc2"), ps("pc3"), ps("ptr"), ps("psc"), ps("po6"), ps("po7")
        PO = [po6, po7]
        ptr_bf = ptr[:].bitcast(BF16)

        cst = sb("cst", [128, C_TOT])
        ident_bf = sb("ident_bf", [128, 128], BF16)
        w_all = sb("w_all", [128, 8, WCOLS], BF16)
        w_out = sb("w_out_bf", [128, 8, D], BF16)
        wr = sb("wr", [128, 8, 72])
        stg = [sb("stg0", [128, 2048]), sb("stg1", [128, 2048])]
        lb = sb("lb", [128, 512])
        oml = sb("oml", [128, 512])
        lbl = sb("lbl", [128, 2, 512])
        gain = sb("gain", [128, 1024])
        ln1g = sb("ln1g", [128, D])
        ln1b = sb("ln1b", [128, D])
        b_a = sb("b_a", [128, 256])
        wgaT = sb("wgaT", [128, D])
        wa2 = sb("wa2", [16, 256])
        LG = sb("LG", [128, NT, 72])

        ident = cst[:, C_ID:C_ID + 128]

        LV = SETUP_LEVEL
        P.dma("sp", lambda e: e.dma_start(out=cst[:], in_=cst_d), writes=["cst"], key="cst")
        P.op("dve", lambda e: e.tensor_copy(out=ident_bf[:], in_=ident), reads=["cst"], writes=["ident_bf"])

        def bload(dst, src, n, key):
            P.dma("sp", lambda e: e.dma_start(out=dst[:], in_=src[0:1, :].to_broadcast([128, n])),
                  writes=[key], key=key)

        if LV >= 2:
          bload(gain, gain_d, 1024, "gain")
        bload(ln1g, ln1g_d, D, "ln1g")
        bload(ln1b, ln1b_d, D, "ln1b")
        bload(b_a, b_a_d, 256, "b_a")
        P.dma("sp", lambda e: e.dma_start(out=lbl[:, 0, :], in_=lbl_d[0:1, :].to_broadcast([128, 512])),
              writes=["lbl0"], key="lbl0")
        P.dma("sp", lambda e: e.dma_start(out=lbl[:, 1, :], in_=lbl_d[1:2, :].to_broadcast([128, 512])),
              writes=["lbl1"], key="lbl1")
        P.dma("sp", lambda e: e.dma_start(out=wgaT[0:16, :], in_=w_gaT_d), writes=["wgaT"], key="wgaT")
        P.dma("sp", lambda e: e.dma_start(out=wa2[:], in_=w_a2_d), writes=["wa2"], key="wa2")
        P.dma("sp", lambda e: e.dma_start(out=wr[:], in_=wr_d.rearrange("(k p) n -> p k n", p=128)),
              writes=["wr"], key="wr")
        P.op("dve", lambda e: e.tensor_tensor(out=lb[:], in0=lbl[:, 1, :], in1=lbl[:, 0, :], op=ALU.subtract),
             reads=["lbl0", "lbl1"], writes=["lb"])
        P.op("act", lambda e: e.activation(out=lb[:], in_=lb[:], func=AF.Exp), reads=["lb"], writes=["lb"])
        P.op("dve", lambda e: e.tensor_scalar(out=lb[:], in0=lb[:], scalar1=1.0, scalar2=None, op0=ALU.add),
             reads=["lb"], writes=["lb"])
        P.op("dve", lambda e: e.reciprocal(out=lb[:], in_=lb[:]), reads=["lb"], writes=["lb"])
        P.op("dve", lambda e: e.tensor_scalar(out=oml[:], in0=lb[:], scalar1=-1.0, scalar2=1.0,
                                              op0=ALU.mult, op1=ALU.add), reads=["lb"], writes=["oml"])

        si = 0
        for k in range(8 if LV >= 4 else 0):
            for hf_ in range(2):
                sg = stg[si % 2]
                skey = "stg%d" % (si % 2)
                c0 = hf_ * 1792
                P.dma("sp", (lambda sg=sg, k=k, c0=c0: lambda e: e.dma_start(
                    out=sg[:, 0:1792], in_=w_in_d[k * 128:(k + 1) * 128, c0:c0 + 1792]))(),
                    writes=[skey], key=skey)
                eng = "pool" if si % 2 == 0 else "act"
                if hf_ == 0:
                    def fn(e, sg=sg, k=k, eng=eng):
                        cp = e.tensor_copy if eng == "pool" else (lambda out, in_: e.activation(out=out, in_=in_, func=AF.Copy))
                        cp(out=w_all[:, k, 0:512], in_=sg[:, 0:512])
                        return cp(out=w_all[:, k, 768:2048], in_=sg[:, 512:1792])
                else:
                    def fn(e, sg=sg, k=k, eng=eng):
                        cp = e.tensor_copy if eng == "pool" else (lambda out, in_: e.activation(out=out, in_=in_, func=AF.Copy))
                        return cp(out=w_all[:, k, 2048:3840], in_=sg[:, 0:1792])
                P.op(eng, fn, reads=[skey], writes=["w_all_%d_%d" % (k, hf_)])
                si += 1
        WALL_KEYS = ["w_all_%d_%d" % (k, h) for k in range(8) for h in range(2)] + ["w_all_z"]
        for k in range(8 if LV >= 5 else 0):
            sg = stg[si % 2]
            skey = "stg%d" % (si % 2)
            P.dma("sp", (lambda sg=sg, k=k: lambda e: e.dma_start(
                out=sg[:, 0:1024], in_=w_out_d[k * 128:(k + 1) * 128, :]))(), writes=[skey], key=skey)
            eng = "pool" if si % 2 == 0 else "act"

            def fn(e, sg=sg, k=k, eng=eng):
                cp = e.tensor_copy if eng == "pool" else (lambda out, in_: e.activation(out=out, in_=in_, func=AF.Copy))
                return cp(out=w_out[:, k, :], in_=sg[:, 0:1024])
            P.op(eng, fn, reads=[skey], writes=["w_out_%d" % k])
            si += 1
        WOUT_KEYS = ["w_out_%d" % k for k in range(8)]
        for k2 in range(4 if LV >= 6 else 0):
            bank = PB[k2 % 2]
            bkey = "pb%d" % (k2 % 2)

            def fn(e, bank=bank, k2=k2):
                ins = None
                for j in range(2):
                    k = 2 * k2 + j
                    ins = e.matmul(bank[:, 256 * j:256 * j + 256], lhsT=wgaT[0:16, k * 128:(k + 1) * 128],
                                   rhs=wa2[:, :], start=True, stop=True)
                return ins
            P.op("pe", fn, reads=["wgaT", "wa2"], writes=[bkey])

            def fn2(e, bank=bank, k2=k2):
                ins = None
                for j in range(2):
                    k = 2 * k2 + j
                    ins = e.tensor_copy(out=w_all[:, k, 512:768], in_=bank[:, 256 * j:256 * j + 256])
                return ins
            P.op("dve", fn2, reads=[bkey], writes=["w_all_z%d" % k2])
        WALL_KEYS = ["w_all_%d_%d" % (k, h) for k in range(8) for h in range(2)] + ["w_all_z%d" % k for k in range(4)]

        xTs = sb("xTs", [128, 8, 128])
        xTb = sb("xTb", [128, 8, 128], BF16)
        xt = sb("xt", [128, D])
        sig = sb("sig", [128, 512])
        omf = sb("omf", [128, 512])
        lgh = sb("lgh", [128, 512])
        zb = sb("zb", [128, 256])
        gkf = sb("gkf", [128, 256])
        spg = sb("spg", [128, 256])
        eb = sb("eb", [128, 768])
        enb = sb("enb", [128, 768])
        erb = sb("erb", [128, 2, 768], BF16)
        gdec = sb("gdec", [128, 12])
        qk = sb("qk", [128, 1792], BF16)
        kh = sb("kh", [128, 2, 768], BF16)
        vv = sb("vv", [128, 1024], BF16)
        sil = sb("sil", [128, 1024])
        qkT = sb("qkT", [128, 14, 128], BF16)
        ATf = sb("ATf", [128, 8, 128], BF16)
        Sf = sb("Sf", [128, 6, 128])
        Sb = sb("Sb", [128, 6, 128], BF16)
        sqj = sb("sqj", [128, 8, 128], BF16)
        ss = sb("ss", [128, 8])
        rstd = sb("rstd", [128, 8])
        on = sb("on", [128, 1024], BF16)
        onT = sb("onT", [128, 8, 128], BF16)
        yy = sb("yy", [128, D])
        bst = sb("bst", [128, 12])
        mv = sb("mv", [128, 2])
        rs1 = sb("rs1", [128, 1])
        x1 = yy
        x1T = sb("x1T", [128, 8, 128])

        if NTILES < NT:
            P.op("dve", lambda e: e.memset(LG[:], 0.0), writes=["LG_%d" % i for i in range(NT)])
        P.op("dve", lambda e: e.memset(Sf[:], 0.0), writes=["Sf"])
        P.op("pool", lambda e: e.memset(qk[:], 0.0), writes=["qk_zero"])
        P.op("pool", lambda e: e.memset(ATf[:], 0.0), writes=["AT_zero"])
        cbias = cst[:, C_CB:C_CB + 2]
        P.op("pool", lambda e: e.memset(Sb[:], 0.0), writes=["Sb"])

        GROUPS = [(0, 512), (512, 1024), (1024, 1536), (1536, 1792), (1792, 2304), (2304, 2816),
                  (2816, 3328), (3328, 3840)]
        maskrep = cst[:, C_MASK:C_MASK + 512]
        L64 = cst[:, C_L:C_L + 128]
        U64 = cst[:, C_U:C_U + 128]
        LGm = cst[:, C_LG:C_LG + 128]
        UGm = cst[:, C_UG:C_UG + 128]
        ind = cst[:, C_IND:C_IND + 4]

        gcount = [0]

        def proj_group(g):
            c0, c1 = GROUPS[g]
            bi = gcount[0] % 2
            gcount[0] += 1
            bank = PB[bi]

            def fn(e):
                ins = None
                for k in range(8):
                    ins = e.matmul(bank[:, 0:c1 - c0], lhsT=xTb[:, k, :], rhs=w_all[:, k, c0:c1],
                                   start=(k == 0), stop=(k == 7))
                return ins
            P.op("pe", fn, reads=["xTb"] + WALL_KEYS, writes=["pb%d" % bi])
            return bank, "pb%d" % bi

        s0bA = stg[0][:].bitcast(BF16)
        QKT_BUF = [qkT, s0bA[:, 0:1792].rearrange("p (a b) -> p a b", b=128)]
        ATF_BUF = [ATf, s0bA[:, 1792:2816].rearrange("p (a b) -> p a b", b=128)]
        VV_BUF = [vv, s0bA[:, 2816:3840]]
        SIL_BUF = [sil, stg[1][:, 0:1024]]
        KH_BUF = [kh, stg[1][:, 1024:1792].bitcast(BF16).rearrange("p (a b) -> p a b", b=768)]
        GD_BUF = [gdec, stg[1][:, 1792:1804]]
        XT_BUF = [xt, wgaT]
        STG0_B = ["qkT_a_B", "qkT_b_B", "AT_B", "AT_zero_B", "vv_h_B", "vv_g_B"]
        STG1_B = ["sil_h_B", "sil_g_B", "gg_B", "kh_h0_B", "kh_h1_B", "kh_g0_B", "kh_g1_B", "gdec_B"]
        WGAT_B = ["xt_B"]
        P.op("pool", lambda e: e.memset(stg[0][:], 0.0), writes=["stg0"] + STG0_B)
        P.op("pool", lambda e: e.memset(stg[1][:], 0.0), writes=["stg1"] + STG1_B)
        P.op("pool", lambda e: e.memset(wgaT[:], 0.0), writes=["wgaT"] + WGAT_B + ["w_all_z%d" % k for k in range(4)])
        def ln_tail(src, skey, dst, dkey, gt, bt, gk_, bk_):
            def fn(e):
                e.bn_stats(out=bst[:, 0:6], in_=src[:, 0:512])
                return e.bn_stats(out=bst[:, 6:12], in_=src[:, 512:1024])
            P.op("dve", fn, reads=[skey], writes=["bst"])
            P.op("dve", lambda e: e.bn_aggr(out=mv[:], in_=bst[:]), reads=["bst"], writes=["mv"])
            P.op("act", lambda e: e.activation(out=rs1[:], in_=mv[:, 1:2], func=AF.Ln, bias=EPS),
                 reads=["mv"], writes=["rs1"])
            P.op("act", lambda e: e.activation(out=rs1[:], in_=rs1[:], func=AF.Exp, scale=-0.5),
                 reads=["rs1"], writes=["rs1"])
            P.op("dve", lambda e: e.tensor_scalar(out=src[:], in0=src[:], scalar1=mv[:, 0:1], scalar2=rs1[:, 0:1],
                                                  op0=ALU.subtract, op1=ALU.mult),
                 reads=[skey, "mv", "rs1"], writes=[skey])
            P.op("dve", lambda e: e.tensor_tensor(out=dst[:], in0=src[:], in1=gt[:], op=ALU.mult),
                 reads=[skey, gk_], writes=[dkey])
            P.op("pool", lambda e: e.tensor_tensor(out=dst[:], in0=dst[:], in1=bt[:], op=ALU.add),
                 reads=[dkey, bk_], writes=[dkey])
        PO_KEYS = ["po6_0", "po6_1", "po7_0", "po7_1"]
        def front(i):
            pr = i % 2
            sfx = "" if pr == 0 else "_B"
            qkT, ATf, vv, kh, gdec, sil, xt = QKT_BUF[pr], ATF_BUF[pr], VV_BUF[pr], KH_BUF[pr], GD_BUF[pr], SIL_BUF[pr], XT_BUF[pr]
            t0 = i * 128
            t0 = i * 128
            P.dma("pool", (lambda i=i: lambda e: e.dma_start(
                out=xTb[:], in_=xT_d[i].rearrange("p (k t) -> p k t", k=8)))(),
                writes=["xTb"], key="xTb")
            P.dma("sp", (lambda t0=t0: lambda e: e.dma_start(out=xt[:], in_=x_d[t0:t0 + 128, :]))(),
                  writes=[("xt" + sfx)], key=("xt" + sfx))
            if BODY_LEVEL < -2:
                return
            bank, bk = proj_group(0)
            P.op("act", (lambda bank=bank: lambda e: e.activation(out=sig[:], in_=bank[:, 0:512], func=AF.Sigmoid))(),
                 reads=[bk], writes=["sig"])
            P.op("dve", lambda e: e.tensor_tensor(out=sig[:], in0=sig[:], in1=oml[:], op=ALU.mult),
                 reads=["sig", "oml"], writes=["sig"])
            P.op("pool", lambda e: e.tensor_tensor(out=lgh[:], in0=sig[:], in1=lb[:], op=ALU.add),
                 reads=["sig", "lb"], writes=["lgh"])
            P.op("pool", lambda e: e.tensor_tensor(out=omf[:], in0=oml[:], in1=sig[:], op=ALU.subtract),
                 reads=["sig", "oml"], writes=["omf"])
            if BODY_LEVEL < -1:
                return
            bank, bk = proj_group(1)
            P.op("dve", (lambda bank=bank: lambda e: e.tensor_tensor(out=zb[:], in0=bank[:, 0:256], in1=b_a[:],
                                                                     op=ALU.add))(),
                 reads=[bk, "b_a"], writes=["zb"])
            if VARIANT == 1:
                P.op("dve", (lambda bank=bank: lambda e: e.tensor_copy(out=gkf[:], in_=bank[:, 256:512]))(),
                     reads=[bk], writes=["gkf"])
            else:
                P.op("act", (lambda bank=bank: lambda e: e.activation(out=gkf[:], in_=bank[:, 256:512], func=AF.Copy))(),
                     reads=[bk] + (["zb"] if VARIANT == 2 else []), writes=["gkf"])
            if BODY_LEVEL < 0:
                return
            P.op("act", lambda e: e.activation(out=lgh[:], in_=lgh[:], func=AF.Ln), reads=["lgh"], writes=["lgh"])
            P.op("act", lambda e: e.activation(out=zb[:], in_=zb[:], func=AF.Exp, scale=-1.0),
                 reads=["zb"], writes=["zb"])
            P.op("act", lambda e: e.activation(out=spg[:], in_=zb[:], func=AF.Ln, bias=1.0),
                 reads=["zb"], writes=["spg"])
            if BODY_LEVEL < 1:
                return
            P.op("pe", lambda e: e.matmul(pc2[:, 0:512], lhsT=L64, rhs=lgh[:], start=True, stop=True),
                 reads=["cst", "lgh"], writes=["pc2"])

            def fn(e):
                e.matmul(pc3[:, 0:256], lhsT=LGm, rhs=spg[:], start=True, stop=True)
                return e.matmul(pc3[:, 256:512], lhsT=UGm, rhs=spg[:], start=True, stop=True)
            P.op("pe", fn, reads=["cst", "spg"], writes=["pc3"])

            def fn(e):
                ins = None
                for h in range(4):
                    ins = e.matmul(psc[:, 2 * h:2 * h + 2], lhsT=lgh[:, 128 * h:128 * h + 128], rhs=ind[:, 0:2],
                                   start=True, stop=True)
                for j in range(2):
                    ins = e.matmul(psc[:, 8 + 2 * j:10 + 2 * j], lhsT=spg[:, 128 * j:128 * j + 128],
                                   rhs=ind[:, 2:4], start=True, stop=True)
                return ins
            P.op("pe", fn, reads=["cst", "lgh", "spg"], writes=["psc"])
            P.op("act", lambda e: e.activation(out=gdec[:], in_=psc[:, 0:12], func=AF.Exp),
                 reads=["psc"], writes=[("gdec" + sfx)])
            P.op("act", lambda e: e.activation(out=eb[:, 0:512], in_=pc2[:, 0:512], func=AF.Exp),
                 reads=["pc2"], writes=["eb_h"])
            P.op("act", lambda e: e.activation(out=enb[:, 0:512], in_=pc2[:, 0:512], func=AF.Exp, scale=-1.0),
                 reads=["pc2"], writes=["enb_h"])
            P.op("act", lambda e: e.activation(out=eb[:, 512:768], in_=pc3[:, 0:256], func=AF.Exp),
                 reads=["pc3"], writes=["eb_g"])
            P.op("act", lambda e: e.activation(out=enb[:, 512:768], in_=pc3[:, 0:256], func=AF.Exp, scale=-1.0),
                 reads=["pc3"], writes=["enb_g"])
            for c in range(2):
                P.op("act", (lambda c=c: lambda e: e.activation(out=erb[:, c, 512:768], in_=pc3[:, 256:512], func=AF.Exp,
                                                                bias=cbias[:, c:c + 1]))(),
                     reads=["pc3", "cst"], writes=["erb_g%d" % c])
            P.op("pe", lambda e: e.matmul(pc2[:, 0:512], lhsT=U64, rhs=lgh[:], start=True, stop=True),
                 reads=["cst", "lgh"], writes=["pc2"])
            for c in range(2):
                P.op("act", (lambda c=c: lambda e: e.activation(out=erb[:, c, 0:512], in_=pc2[:, 0:512], func=AF.Exp,
                                                                bias=cbias[:, c:c + 1]))(),
                     reads=["pc2", "cst"], writes=["erb_h%d" % c])
            if BODY_LEVEL < 2:
                return
            bank, bk = proj_group(2)
            P.op("dve", (lambda bank=bank: lambda e: e.scalar_tensor_tensor(
                out=qk[:, 0:512], in0=bank[:, 0:512], scalar=128.0 ** -0.5, in1=eb[:, 0:512],
                op0=ALU.mult, op1=ALU.mult))(), reads=[bk, "eb_h"], writes=["qk_qh"])
            bank, bk = proj_group(3)

            def fn(e, bank=bank):
                ins = None
                qv = qk[:, 512:1024].rearrange("p (j x) -> p j x", j=2)
                bv = bank[:, 0:256].rearrange("p (j r f) -> p j r f", j=2, r=2)
                ev = eb[:, 512:768].rearrange("p (j r f) -> p j r f", j=2, r=2)
                for r in range(2):
                    ins = e.scalar_tensor_tensor(out=qv[:, :, 192 * r:192 * r + 64], in0=bv[:, :, r, :],
                                                 scalar=64.0 ** -0.5, in1=ev[:, :, r, :], op0=ALU.mult, op1=ALU.mult)
                return ins
            P.op("dve", fn, reads=[bk, "eb_g", "qk_zero"], writes=["qk_qg"])
            P.op("pool", lambda e: e.tensor_tensor(out=qk[:, 1024:1536], in0=omf[:], in1=enb[:, 0:512], op=ALU.mult),
                 reads=["omf", "enb_h"], writes=["qk_kh"])
            P.op("pool", lambda e: e.tensor_tensor(out=qk[:, 1536:1792], in0=gkf[:], in1=enb[:, 512:768], op=ALU.mult),
                 reads=["gkf", "enb_g"], writes=["qk_kg"])
            for c in range(2):
                P.op("pool", (lambda c=c: lambda e: e.tensor_tensor(out=kh[:, c, 0:512], in0=omf[:], in1=erb[:, c, 0:512],
                                                                    op=ALU.mult))(),
                     reads=["omf", "erb_h%d" % c], writes=[("kh_h%d" % c + sfx)])
                P.op("pool", (lambda c=c: lambda e: e.tensor_tensor(out=kh[:, c, 512:768], in0=gkf[:],
                                                                    in1=erb[:, c, 512:768], op=ALU.mult))(),
                     reads=["gkf", "erb_g%d" % c], writes=[("kh_g%d" % c + sfx)])
            bank, bk = proj_group(4)
            P.op("act", (lambda bank=bank: lambda e: e.activation(out=vv[:, 0:512], in_=bank[:, 0:512], func=AF.Copy))(),
                 reads=[bk], writes=[("vv_h" + sfx)])
            bank, bk = proj_group(5)
            P.op("act", (lambda bank=bank: lambda e: e.activation(out=vv[:, 512:1024], in_=bank[:, 0:512], func=AF.Copy))(),
                 reads=[bk], writes=[("vv_g" + sfx)])
            bank, bk = proj_group(6)
            P.op("act", (lambda bank=bank: lambda e: e.activation(out=sil[:, 0:512], in_=bank[:, 0:512],
                                                                  func=AF.Silu))(), reads=[bk], writes=[("sil_h" + sfx), ("gg" + sfx)])
            bank, bk = proj_group(7)
            P.op("act", (lambda bank=bank: lambda e: e.activation(out=sil[:, 512:1024], in_=bank[:, 0:512],
                                                                  func=AF.Silu))(), reads=[bk], writes=[("sil_g" + sfx), ("gg" + sfx)])
            P.op("pool", lambda e: e.tensor_tensor(out=sil[:], in0=sil[:], in1=gain[:], op=ALU.mult),
                 reads=[("sil_h" + sfx), ("sil_g" + sfx), "gain"], writes=[("gg" + sfx), ("sil_h" + sfx), ("sil_g" + sfx)])
            if BODY_LEVEL < 3:
                return
            QK_KEYS = ["qk_qh", "qk_qg", "qk_kh", "qk_kg"]

            def fn(e):
                ins = None
                for j in range(8):
                    ins = e.transpose(ptr_bf[:, 128 * j:128 * j + 128], qk[:, 128 * j:128 * j + 128], ident_bf[:])
                return ins
            P.op("pe", fn, reads=QK_KEYS + ["ident_bf"], writes=["ptr"])
            P.op("dve", lambda e: e.tensor_copy(out=qkT[:, 0:8, :], in_=ptr_bf[:, 0:1024].rearrange("p (a b) -> p a b", b=128)),
                 reads=["ptr"], writes=[("qkT_a" + sfx)])

            def fn(e):
                ins = None
                for j in range(6):
                    ins = e.transpose(ptr_bf[:, 128 * j:128 * j + 128], qk[:, 1024 + 128 * j:1024 + 128 * j + 128],
                                      ident_bf[:])
                return ins
            P.op("pe", fn, reads=QK_KEYS + ["ident_bf"], writes=["ptr"])
            P.op("dve", lambda e: e.tensor_copy(out=qkT[:, 8:14, :], in_=ptr_bf[:, 0:768].rearrange("p (a b) -> p a b", b=128)),
                 reads=["ptr"], writes=[("qkT_b" + sfx)])
            QKT = [("qkT_a" + sfx), ("qkT_b" + sfx)]
            if BODY_LEVEL < 4:
                return

            def fn(e):
                ins = None
                for c in range(2):
                    cs = slice(64 * c, 64 * c + 64)
                    for h in range(4):
                        ins = e.matmul(psc[cs, 64 * h:64 * h + 64], lhsT=qkT[:, 8 + h, cs], rhs=qkT[:, h, cs],
                                       start=True, stop=True)
                    for g in range(4):
                        ins = e.matmul(psc[cs, 64 * (4 + g):64 * (4 + g) + 64], lhsT=qkT[:, 12 + g // 2, cs],
                                       rhs=qkT[:, 4 + g, cs], start=True, stop=True)
                return ins
            P.op("pe", fn, reads=QKT, writes=["psc"])

            def fn(e):
                ins = None
                for c in range(2):
                    cs = slice(64 * c, 64 * c + 64)
                    ins = e.tensor_tensor(out=ATf[cs, :, 64 * c:64 * c + 64],
                                          in0=psc[cs, 0:512].rearrange("p (h t) -> p h t", t=64),
                                          in1=maskrep[cs, :].rearrange("p (h t) -> p h t", t=64), op=ALU.mult)
                return ins
            P.op("dve", fn, reads=["psc", "cst", ("AT_zero" + sfx)], writes=[("AT" + sfx)])
            if BODY_LEVEL < 5:
                return
        def back(i):
            pr = i % 2
            sfx = "" if pr == 0 else "_B"
            qkT, ATf, vv, kh, gdec, sil, xt = QKT_BUF[pr], ATF_BUF[pr], VV_BUF[pr], KH_BUF[pr], GD_BUF[pr], SIL_BUF[pr], XT_BUF[pr]
            t0 = i * 128
            QKT = [("qkT_a" + sfx), ("qkT_b" + sfx)]
            for c in range(2):
                cs = slice(64 * c, 64 * c + 64)

                def fn(e, c=c, cs=cs):
                    ins = None
                    for bh in range(8):
                        po = PO[bh // 4]
                        col = 128 * (bh % 4)
                        e.matmul(po[cs, col:col + 128], lhsT=ATf[:, bh, cs], rhs=vv[:, 128 * bh:128 * bh + 128],
                                 start=True, stop=False)
                        sblk = bh if bh < 4 else 4 + (bh - 4) // 2
                        ins = e.matmul(po[cs, col:col + 128], lhsT=qkT[:, bh, cs], rhs=Sb[:, sblk, :],
                                       start=False, stop=True)
                    return ins
                P.op("pe", fn, reads=[("AT" + sfx), ("vv_h" + sfx), ("vv_g" + sfx), "Sb"] + QKT, writes=["po6_%d" % c, "po7_%d" % c])

                def fn(e, c=c):
                    ins = None
                    for h in range(4):
                        ins = e.matmul(pc2[:, 128 * h:128 * h + 128], lhsT=kh[:, c, 128 * h:128 * h + 128],
                                       rhs=vv[:, 128 * h:128 * h + 128], start=True, stop=True)
                    for g in range(4):
                        j, r = g // 2, g % 2
                        ins = e.matmul(pc3[64 * r:64 * r + 64, 128 * j:128 * j + 128],
                                       lhsT=kh[:, c, 512 + 64 * g:512 + 64 * g + 64],
                                       rhs=vv[:, 512 + 128 * g:512 + 128 * g + 128], start=True, stop=True)
                    return ins
                P.op("pe", fn, reads=[("kh_h%d" % c + sfx), ("kh_g%d" % c + sfx), ("vv_h" + sfx), ("vv_g" + sfx)], writes=["pc2", "pc3"])

                def fn(e, c=c):
                    ins = None
                    for h in range(4):
                        ins = e.scalar_tensor_tensor(out=Sf[:, h, :], in0=Sf[:, h, :],
                                                     scalar=gdec[:, 2 * h + c:2 * h + c + 1],
                                                     in1=pc2[:, 128 * h:128 * h + 128], op0=ALU.mult, op1=ALU.add)
                    for j in range(2):
                        ins = e.scalar_tensor_tensor(out=Sf[:, 4 + j, :], in0=Sf[:, 4 + j, :],
                                                     scalar=gdec[:, 8 + 2 * j + c:8 + 2 * j + c + 1],
                                                     in1=pc3[:, 128 * j:128 * j + 128], op0=ALU.mult, op1=ALU.add)
                    return ins
                P.op("dve", fn, reads=["Sf", ("gdec" + sfx), "pc2", "pc3"], writes=["Sf"])
                P.op("act", lambda e: e.activation(out=Sb[:], in_=Sf[:], func=AF.Copy), reads=["Sf"], writes=["Sb"])
            if BODY_LEVEL < 6:
                return
            for bh in range(8):
                pob = PO[bh // 4][:, 128 * (bh % 4):128 * (bh % 4) + 128]
                P.op("act", (lambda pob=pob, bh=bh: lambda e: e.activation(
                    out=sqj[:, bh, :], in_=pob, func=AF.Square, accum_out=ss[:, bh:bh + 1]))(),
                    reads=PO_KEYS, writes=["ss%d" % bh, "sqj%d" % bh])
            SS = ["ss%d" % b for b in range(8)]
            P.op("act", lambda e: e.activation(out=rstd[:], in_=ss[:], func=AF.Ln, scale=1.0 / 128.0, bias=EPS),
                 reads=SS, writes=["rstd"])
            P.op("act", lambda e: e.activation(out=rstd[:], in_=rstd[:], func=AF.Exp, scale=-0.5),
                 reads=["rstd"], writes=["rstd"])

            def fn(e):
                ins = None
                for bh in range(8):
                    pob = PO[bh // 4][:, 128 * (bh % 4):128 * (bh % 4) + 128]
                    ins = e.scalar_tensor_tensor(out=on[:, 128 * bh:128 * bh + 128], in0=pob,
                                                 scalar=rstd[:, bh:bh + 1], in1=sil[:, 128 * bh:128 * bh + 128],
                                                 op0=ALU.mult, op1=ALU.mult)
                return ins
            P.op("dve", fn, reads=PO_KEYS + ["rstd", ("gg" + sfx)], writes=["on"])

            def fn(e):
                ins = None
                for j in range(8):
                    ins = e.transpose(ptr_bf[:, 128 * j:128 * j + 128], on[:, 128 * j:128 * j + 128], ident_bf[:])
                return ins
            P.op("pe", fn, reads=["on", "ident_bf"], writes=["ptr"])
            P.op("dve", lambda e: e.tensor_copy(out=onT[:], in_=ptr_bf[:, 0:1024].rearrange("p (a b) -> p a b", b=128)), reads=["ptr"], writes=["onT"])
            if BODY_LEVEL < 7:
                return

            def fn(e):
                ins = None
                for n in range(2):
                    for j in range(8):
                        ins = e.matmul(PO[n][:, 0:512], lhsT=onT[:, j, :], rhs=w_out[:, j, 512 * n:512 * n + 512],
                                       start=(j == 0), stop=(j == 7))
                return ins
            P.op("pe", fn, reads=["onT"] + WOUT_KEYS, writes=PO_KEYS)

            def fn(e):
                ins = None
                for n in range(2):
                    ins = e.scalar_tensor_tensor(out=yy[:, 512 * n:512 * n + 512], in0=xt[:, 512 * n:512 * n + 512],
                                                 scalar=ALPHA, in1=PO[n][:, 0:512], op0=ALU.mult, op1=ALU.add)
                return ins
            P.op("dve", fn, reads=[("xt" + sfx)] + PO_KEYS, writes=["yy"])

            ln_tail(yy, "yy", yy, "yy", ln1g, ln1b, "ln1g", "ln1b")
            if stage == "A":
                P.dma("sp", (lambda t0=t0: lambda e: e.dma_start(out=out_d[t0:t0 + 128, :], in_=x1[:]))(),
                      reads=["yy"], writes=["out"], key="x1st")
                return
            P.dma("sp", (lambda t0=t0: lambda e: e.dma_start(out=X1_d[t0:t0 + 128, :], in_=x1[:]))(),
                  reads=["yy"], writes=["X1_%d" % i], key="x1st")
            if stage != "A" and (3 <= i < 11 or (NTILES < 11 and i == 0)):
                for n8 in ([i - 3] if NTILES >= 11 else range(8)):
                    P.dma("sp", (lambda n8=n8: lambda e: e.dma_start(
                        out=xb_d[1024 * n8:1024 * n8 + 1024, :], in_=zeros_d))(),
                        reads=["yy"], writes=["xbz%d" % n8], key="xbz")
            P.op("pool", lambda e: e.tensor_copy(out=on[:], in_=yy[:]), reads=["yy"], writes=["on"])
            P.dma("sp", (lambda t0=t0: lambda e: e.dma_start(out=X1b_d[t0:t0 + 128, :], in_=on[:]))(),
                  reads=["on"], writes=["X1b_%d" % i], key="x1bst")
            for rnd in range(2):
                def fn(e, rnd=rnd):
                    ins = None
                    for j in range(4):
                        jj = 4 * rnd + j
                        ins = e.transpose(ptr[:, 128 * j:128 * j + 128], yy[:, 128 * jj:128 * jj + 128], ident)
                    return ins
                P.op("pe", fn, reads=["yy", "cst"], writes=["ptr"])
                P.op("dve", (lambda rnd=rnd: lambda e: e.tensor_copy(
                    out=x1T[:, 4 * rnd:4 * rnd + 4, :], in_=ptr[:, 0:512].rearrange("p (a b) -> p a b", b=128)))(),
                    reads=["ptr"], writes=["x1T_%d" % rnd])

            def fn(e):
                ins = None
                for k in range(8):
                    ins = e.matmul(psc[:, 0:72], lhsT=x1T[:, k, :], rhs=wr[:, k, :], start=(k == 0), stop=(k == 7))
                return ins
            P.op("pe", fn, reads=["x1T_0", "x1T_1", "wr"], writes=["psc"])
            P.op("act", (lambda i=i: lambda e: e.activation(out=LG[:, i, :], in_=psc[:, 0:72], func=AF.Copy))(),
                 reads=["psc"], writes=["LG_%d" % i])

        n_tiles = NTILES
        if n_tiles > 0:
            front(0)
        for i in range(n_tiles):
            if i + 1 < n_tiles:
                front(i + 1)
            back(i)

        if stage != "A":
            LGK = ["LG_%d" % i for i in range(NT)]
            NTL = n_tiles
            K_SIL = ["sil_h", "sil_g", "gg"]
            K_EB = ["eb_h", "eb_g"]
            K_ENB = ["enb_h", "enb_g"]
            K_QK = ["qk_qh", "qk_qg", "qk_kh", "qk_kg", "qk_zero"]
            tmp4 = sil[:].rearrange("p (i j g) -> p i j g", i=16, j=8)
            Rt = xt[:].rearrange("p (i e) -> p i e", i=16)
            E1 = lbl[:].rearrange("p a b -> p (a b)").rearrange("p (i e) -> p i e", e=64)
            E2 = gain[:].rearrange("p (i e) -> p i e", e=64)
            cum = xTs[:].rearrange("p a b -> p (a b)").rearrange("p (i e) -> p i e", e=64)
            Cb = on[:]
            sm = zb[:, 0:192].rearrange("p (a b) -> p a b", b=16)
            s8 = x1T[:].rearrange("p a b -> p (a b)").rearrange("p (n i j) -> p n i j", n=8, i=16)
            d0i = spg[:].bitcast(I32)[:, 0:16]
            d1i = spg[:].bitcast(I32)[:, 16:32]
            slb = gkf[:].bitcast(BF16)[:, 0:128]
            oneb = gkf[:].bitcast(BF16)[:, 128:256]
            V = lambda n: sm[:, n, :]
            gmax, gsum, pg, m1, m2, rr, den, gate1, gate2, d0f, d1f = [V(n) for n in range(11)]
            ohg, esub, esel, oh1, msk, oh2 = [s8[:, n, :, :] for n in range(6)]
            lgp = LG[:, :, 0:8]
            P.op("dve", lambda e: e.memset(lbl[:], 0.0), writes=["lbl0", "lbl1", "E1"])
            P.op("dve", lambda e: e.memset(gain[:], 0.0), writes=["gain", "E2"])
            P.op("dve", lambda e: e.memset(xTs[:], 0.0), writes=["xTs", "cum"])
            P.op("dve", lambda e: e.memset(zb[:], 0.0),
                 writes=["zb", "gmax", "gsum", "pg", "m1", "m2", "rr", "den", "gate1", "gate2", "d0f", "d1f"])
            P.op("dve", lambda e: e.memset(x1T[:], 0.0),
                 writes=["x1T_0", "x1T_1", "ohg", "esub", "esel", "oh1", "msk", "oh2"])
            P.op("dve", lambda e: e.memset(spg[:], 0.0), writes=["spg", "d0", "d1"])
            P.op("dve", lambda e: e.memset(gkf[:], 0.0), writes=["gkf", "slb", "oneb"])
            P.op("dve", lambda e: e.memset(wgaT[:], 0.0), writes=["wgaT", "y0"] + WGAT_B)
            P.op("dve", lambda e: e.tensor_copy(out=slb[:], in_=cst[:, C_SL:C_SL + 128]), reads=["cst"], writes=["slb"])
            P.op("dve", lambda e: e.tensor_copy(out=oneb[:], in_=cst[:, C_ONE:C_ONE + 128]), reads=["cst"], writes=["oneb"])

            def bc8(v):
                return v.unsqueeze(2).to_broadcast([128, 16, 8])
            P.op("dve", lambda e: e.tensor_reduce(out=gmax, in_=lgp, axis=AX.X, op=ALU.max), reads=LGK, writes=["gmax"])
            P.op("dve", lambda e: e.tensor_tensor(out=ohg, in0=lgp, in1=bc8(gmax), op=ALU.is_equal),
                 reads=LGK + ["gmax"], writes=["ohg"])
            P.op("dve", lambda e: e.tensor_tensor(out=esub, in0=lgp, in1=bc8(gmax), op=ALU.subtract),
                 reads=LGK + ["gmax"], writes=["esub"])
            P.op("act", lambda e: e.activation(out=esub, in_=esub, func=AF.Exp), reads=["esub"], writes=["esub"])
            P.op("dve", lambda e: e.tensor_reduce(out=gsum, in_=esub, axis=AX.X, op=ALU.add), reads=["esub"], writes=["gsum"])
            P.op("dve", lambda e: e.reciprocal(out=pg, in_=gsum), reads=["gsum"], writes=["pg"])
            P.op("dve", lambda e: e.tensor_tensor(
                out=tmp4, in0=LG[:, :, 8:72].rearrange("p i (g j) -> p i j g", g=8),
                in1=ohg.unsqueeze(2).to_broadcast([128, 16, 8, 8]), op=ALU.mult),
                reads=LGK + ["ohg"], writes=K_SIL)
            P.op("dve", lambda e: e.tensor_reduce(out=esel.rearrange("p i j -> p (i j)"),
                                                  in_=tmp4.rearrange("p i j g -> p (i j) g"), axis=AX.X, op=ALU.add),
                 reads=K_SIL, writes=["esel"])
            P.op("dve", lambda e: e.tensor_reduce(out=m1, in_=esel, axis=AX.X, op=ALU.max), reads=["esel"], writes=["m1"])
            P.op("dve", lambda e: e.tensor_tensor(out=oh1, in0=esel, in1=bc8(m1), op=ALU.is_equal),
                 reads=["esel", "m1"], writes=["oh1"])
            P.op("dve", lambda e: e.scalar_tensor_tensor(out=msk, in0=oh1, scalar=-1.0e30, in1=esel,
                                                         op0=ALU.mult, op1=ALU.add),
                 reads=["oh1", "esel"], writes=["msk"])
            P.op("dve", lambda e: e.tensor_reduce(out=m2, in_=msk, axis=AX.X, op=ALU.max), reads=["msk"], writes=["m2"])
            P.op("dve", lambda e: e.tensor_tensor(out=oh2, in0=msk, in1=bc8(m2), op=ALU.is_equal),
                 reads=["msk", "m2"], writes=["oh2"])
            P.op("dve", lambda e: e.tensor_tensor(out=rr, in0=m2, in1=m1, op=ALU.subtract), reads=["m1", "m2"], writes=["rr"])
            P.op("act", lambda e: e.activation(out=rr, in_=rr, func=AF.Exp), reads=["rr"], writes=["rr"])
            P.op("dve", lambda e: e.tensor_scalar(out=den, in0=rr, scalar1=1.0, scalar2=None, op0=ALU.add),
                 reads=["rr"], writes=["den"])
            P.op("dve", lambda e: e.reciprocal(out=den, in_=den), reads=["den"], writes=["den"])
            P.op("dve", lambda e: e.tensor_tensor(out=gate1, in0=pg, in1=den, op=ALU.mult), reads=["pg", "den"], writes=["gate1"])
            P.op("dve", lambda e: e.tensor_tensor(out=gate2, in0=gate1, in1=rr, op=ALU.mult), reads=["gate1", "rr"], writes=["gate2"])
            E1v = E1[:].rearrange("p i (g j) -> p i g j", g=8)
            E2v = E2[:].rearrange("p i (g j) -> p i g j", g=8)
            P.op("dve", lambda e: e.tensor_tensor(out=E1v, in0=ohg.unsqueeze(3).to_broadcast([128, 16, 8, 8]),
                                                  in1=oh1.unsqueeze(2).to_broadcast([128, 16, 8, 8]), op=ALU.mult),
                 reads=["ohg", "oh1"], writes=["E1"])
            P.op("dve", lambda e: e.tensor_tensor(out=E2v, in0=ohg.unsqueeze(3).to_broadcast([128, 16, 8, 8]),
                                                  in1=oh2.unsqueeze(2).to_broadcast([128, 16, 8, 8]), op=ALU.mult),
                 reads=["ohg", "oh2"], writes=["E2"])
            P.op("dve", lambda e: e.tensor_tensor(out=Cb, in0=E1[:].rearrange("p i e -> p (i e)"),
                                                  in1=E2[:].rearrange("p i e -> p (i e)"), op=ALU.add),
                 reads=["E1", "E2"], writes=["on"])
            RB = [PB[0], PB[1]]
            TB = [pc2, pc3]
            for hh in range(2):
                P.op("pe", (lambda hh=hh: lambda e: e.matmul(RB[hh][:, 0:512], lhsT=slb[:], rhs=Cb[:, 512 * hh:512 * hh + 512],
                                                             start=True, stop=True))(),
                     reads=["slb", "on"], writes=["pb%d" % hh])
                P.op("pe", (lambda hh=hh: lambda e: e.matmul(TB[hh][:, 0:512], lhsT=oneb[:], rhs=Cb[:, 512 * hh:512 * hh + 512],
                                                             start=True, stop=True))(),
                     reads=["oneb", "on"], writes=["pc%d" % (2 + hh)])
            P.op("dve", lambda e: e.memset(cum[:, 0, :], 0.0), writes=["cum"])
            for i in range(1, 16):
                src = TB[(i - 1) // 8][:, 64 * ((i - 1) % 8):64 * ((i - 1) % 8) + 64]
                P.op("dve", (lambda i=i, src=src: lambda e: e.tensor_tensor(out=cum[:, i, :], in0=cum[:, i - 1, :], in1=src,
                                                                            op=ALU.add))(),
                     reads=["cum", "pc2", "pc3"], writes=["cum"])
            for hh in range(2):
                P.op("dve", (lambda hh=hh: lambda e: e.tensor_tensor(
                    out=Rt[:, 8 * hh:8 * hh + 8, :], in0=RB[hh][:, 0:512].rearrange("p (i e) -> p i e", e=64),
                    in1=cum[:, 8 * hh:8 * hh + 8, :], op=ALU.add))(),
                    reads=["pb%d" % hh, "cum"], writes=["xt"])
            P.op("dve", lambda e: e.tensor_scalar(out=cum[:], in0=Rt, scalar1=float(CAP), scalar2=1.0e6,
                                                  op0=ALU.is_ge, op1=ALU.mult), reads=["xt"], writes=["cum"])
            P.op("dve", lambda e: e.tensor_tensor(out=Rt, in0=Rt, in1=cum[:], op=ALU.add), reads=["xt", "cum"], writes=["xt"])
            P.op("dve", lambda e: e.tensor_tensor(out=Rt, in0=Rt,
                                                  in1=cst[:, C_EB:C_EB + 64].unsqueeze(1).to_broadcast([128, 16, 64]),
                                                  op=ALU.add), reads=["xt", "cst"], writes=["xt"])
            for kk, (Ek, ekey, df, di, dkey) in enumerate([(E1, "E1", d0f, d0i, "d0"), (E2, "E2", d1f, d1i, "d1")]):
                P.op("dve", (lambda Ek=Ek: lambda e: e.tensor_tensor(out=Ek[:], in0=Ek[:], in1=Rt, op=ALU.mult))(),
                     reads=[ekey, "xt"], writes=[ekey])
                P.op("dve", (lambda Ek=Ek, df=df: lambda e: e.tensor_reduce(out=df, in_=Ek[:], axis=AX.X, op=ALU.add))(),
                     reads=[ekey], writes=[dkey + "f"])
                P.op("dve", (lambda df=df, di=di: lambda e: e.tensor_copy(out=di[:], in_=df))(),
                     reads=[dkey + "f"], writes=[dkey])

            x1b = vv
            K_VV = ["vv_h", "vv_g"]
            XBZ = ["xbz%d" % n for n in range(8)]
            c4 = [vv, VV_BUF[1], on, ATF_BUF[0][:].rearrange("p a b -> p (a b)")]
            c4k = [K_VV, ["vv_h_B", "vv_g_B"], ["on"], ["AT", "AT_zero"]]
            for i in range(NTL):
                t0 = i * 128
                cb, ck = c4[i % 4], c4k[i % 4]
                P.dma("sp", (lambda t0=t0, cb=cb: lambda e: e.dma_start(out=cb[:], in_=X1b_d[t0:t0 + 128, :]))(),
                      reads=["X1b_%d" % i], writes=ck, key="c4ld%d" % (i % 4))
                for kk, (di, dkey) in enumerate([(d0i, "d0"), (d1i, "d1")]):
                    P.dma("pool", (lambda i=i, di=di, cb=cb: lambda e: e.indirect_dma_start(
                        out=xb_d[:, :], out_offset=bass.IndirectOffsetOnAxis(ap=di[:, i:i + 1], axis=0),
                        in_=cb[:], in_offset=None, bounds_check=breg(e), oob_is_err=False))(),
                        reads=ck + [dkey] + XBZ, writes=["xbs_%d_%d" % (i, kk)], key="scat%d" % (i % 4))
            XBS = ["xbs_%d_%d" % (i, kk) for i in range(NTL) for kk in range(2)]

            wflat = w_all[:].rearrange("p k c -> p (k c)")
            woflat = w_out[:].rearrange("p k c -> p (k c)")
            wgu = [wflat[:, 8192 * b:8192 * b + 8192].rearrange("p (a n) -> p a n", n=512) for b in range(3)]
            wdn = [wflat[:, 24576:28672].rearrange("p (a n) -> p a n", n=1024),
                   woflat[:, 0:4096].rearrange("p (a n) -> p a n", n=1024),
                   woflat[:, 4096:8192].rearrange("p (a n) -> p a n", n=1024)]
            s0b = stg[0][:].bitcast(BF16)
            s1b = stg[1][:].bitcast(BF16)
            xblk = [s0b[:, 0:1024], s0b[:, 1024:2048], s1b[:, 1024:2048], s1b[:, 2048:3072]]
            xbT = [s0b[:, 2048:3072].rearrange("p (a b) -> p a b", b=128), s0b[:, 3072:4096].rearrange("p (a b) -> p a b", b=128)]
            hT = [s1b[:, 0:512].rearrange("p (a b) -> p a b", b=128), s1b[:, 512:1024].rearrange("p (a b) -> p a b", b=128)]
            sgt = [eb[:, 0:512], enb[:, 0:512]]
            K_SG = [K_EB, K_ENB]
            yblk = [sil, yy]
            K_YB = [K_SIL, ["yy"]]
            GU = [(PB[0], PB[1], "pb0", "pb1"), (pc2, pc3, "pc2", "pc3")]
            NEXP = NE

            def wload(ex):
                b = ex % 3
                ex_gu = WALL_KEYS if ex < 3 else []
                ex_dn = (WALL_KEYS if b == 0 else WOUT_KEYS) if ex < 3 else []
                P.dma("pool", (lambda ex=ex, b=b: lambda e: e.dma_start(
                    out=wgu[b].rearrange("p a n -> p (a n)").rearrange("p (c m) -> p c m", m=1024),
                    in_=wgu_d[ex].rearrange("p (c m) -> p c m", m=1024)))(),
                    writes=["wgu%d" % b] + ex_gu, key="wgu%d" % b)
                P.dma("pool", (lambda ex=ex, b=b: lambda e: e.dma_start(
                    out=wdn[b].rearrange("p a n -> p (a n)").rearrange("p (c m) -> p c m", m=1024),
                    in_=wdn_d[ex].rearrange("p (c m) -> p c m", m=1024)))(),
                    writes=["wdn%d" % b] + ex_dn, key="wdn%d" % b)
            P.op("dve", lambda e: e.memset(stg[0][:], 0.0), writes=["stg0", "xblk0", "xblk1", "xbT0", "xbT1"] + STG0_B)
            P.op("dve", lambda e: e.memset(stg[1][:], 0.0), writes=["stg1", "hT0", "hT1", "xblk2", "xblk3"] + STG1_B)
            def xload(ex):
                q4 = ex % 4
                P.dma("sp", (lambda ex=ex, q4=q4: lambda e: e.dma_start(out=xblk[q4], in_=xb_d[ex * 128:(ex + 1) * 128, :]))(),
                      reads=XBS + XBZ, writes=["xblk%d" % q4], key="xblk%d" % q4)

            def stage1(ex):
                b = ex % 3
                q = ex % 2
                q4 = ex % 4

                def fn(e, q4=q4):
                    ins = None
                    for j in range(8):
                        ins = e.transpose(ptr_bf[:, 128 * j:128 * j + 128], xblk[q4][:, 128 * j:128 * j + 128], ident_bf[:])
                    return ins
                P.op("pe", fn, reads=["xblk%d" % q4, "ident_bf"], writes=["ptr"])
                P.op("dve", (lambda q=q: lambda e: e.tensor_copy(out=xbT[q], in_=ptr_bf[:, 0:1024].rearrange("p (a b) -> p a b", b=128)))(),
                     reads=["ptr"], writes=["xbT%d" % q])
                gb, ub, gk_, uk_ = GU[q]
                for m, (bank, bkey) in enumerate([(gb, gk_), (ub, uk_)]):
                    def fn(e, m=m, b=b, q=q, bank=bank):
                        ins = None
                        for nchunk in range(4):
                            for k in range(8):
                                ins = e.matmul(bank[:, 128 * nchunk:128 * nchunk + 128],
                                               lhsT=wgu[b][:, 8 * m + k, 128 * nchunk:128 * nchunk + 128],
                                               rhs=xbT[q][:, k, :], start=(k == 0), stop=(k == 7))
                        return ins
                    P.op("pe", fn, reads=["xbT%d" % q, "wgu%d" % b], writes=[bkey])
                P.op("act", (lambda q=q, gb=gb: lambda e: e.activation(out=sgt[q], in_=gb[:, 0:512], func=AF.Silu))(),
                     reads=[gk_], writes=K_SG[q])
                P.op("dve", (lambda q=q, ub=ub: lambda e: e.tensor_tensor(
                    out=hT[q].rearrange("p a b -> p (a b)"), in0=sgt[q], in1=ub[:, 0:512], op=ALU.mult))(),
                    reads=K_SG[q] + [uk_], writes=["hT%d" % q])

            def stage2(ex):
                b = ex % 3
                q = ex % 2

                def fn(e, b=b, q=q):
                    ins = None
                    for half in range(2):
                        for nchunk in range(4):
                            ins = e.matmul(PO[half][:, 0:512], lhsT=hT[q][:, nchunk, :],
                                           rhs=wdn[b][:, nchunk, 512 * half:512 * half + 512],
                                           start=(nchunk == 0), stop=(nchunk == 3))
                    return ins
                P.op("pe", fn, reads=["hT%d" % q, "wdn%d" % b], writes=PO_KEYS)
                P.op("act", (lambda q=q: lambda e: e.activation(out=yblk[q][:, 0:512], in_=po6[:, 0:512], func=AF.Copy))(),
                     reads=PO_KEYS, writes=K_YB[q])
                P.op("dve", (lambda q=q: lambda e: e.tensor_copy(out=yblk[q][:, 512:1024], in_=po7[:, 0:512]))(),
                     reads=PO_KEYS, writes=K_YB[q])
                P.dma("act", (lambda ex=ex, q=q: lambda e: e.dma_start(out=yb_d[ex * 128:(ex + 1) * 128, :], in_=yblk[q][:]))(),
                      reads=K_YB[q], writes=["yb_%d" % ex], key="ybst%d" % q)

            for ex0 in range(3):
                xload(ex0)
            wload(0)
            wload(1)
            stage1(0)
            for ex in range(NEXP):
                if ex + 2 < NEXP:
                    wload(ex + 2)
                if ex + 3 < NEXP:
                    xload(ex + 3)
                if ex + 1 < NEXP:
                    stage1(ex + 1)
                stage2(ex)
            YBK = ["yb_%d" % ex for ex in range(NEXP)]

            P.dma("sp", lambda e: e.dma_start(out=ln1g[:], in_=ln2g_d[0:1, :].to_broadcast([128, D])),
                  writes=["ln1g"], key="ln1g")
            P.dma("sp", lambda e: e.dma_start(out=ln1b[:], in_=ln2b_d[0:1, :].to_broadcast([128, D])),
                  writes=["ln1b"], key="ln1b")
            wf32 = wflat.bitcast(F32)
            WD_KEYS = ["wgu0", "wgu1", "wgu2", "wdn0"]
            EK = ["e_y0_0", "e_y0_1", "e_y1_0", "e_y1_1", "e_x_0", "e_x_1", "e_a_0", "e_a_1"]
            P.op("dve", lambda e: e.memset(wf32[:, 0:8192], 0.0), writes=WD_KEYS + EK)
            EY0 = [wf32[:, 0:1024], wf32[:, 1024:2048]]
            EY1 = [wf32[:, 2048:3072], wf32[:, 3072:4096]]
            EX = [wf32[:, 4096:5120], wf32[:, 5120:6144]]
            EA = [wf32[:, 6144:7168], wf32[:, 7168:8192]]
            for i in range(NTL):
                t0 = i * 128
                q = i % 2
                y0, y1, xq, aq = EY0[q], EY1[q], EX[q], EA[q]
                ky0, ky1, kx, ka = "e_y0_%d" % q, "e_y1_%d" % q, "e_x_%d" % q, "e_a_%d" % q
                P.dma("sp", (lambda t0=t0, xq=xq: lambda e: e.dma_start(out=xq, in_=X1_d[t0:t0 + 128, :]))(),
                      reads=["X1_%d" % i], writes=[kx], key=kx)
                P.dma("pool", (lambda i=i, y0=y0: lambda e: e.indirect_dma_start(
                    out=y0, out_offset=None, in_=yb_d[:, :],
                    in_offset=bass.IndirectOffsetOnAxis(ap=d0i[:, i:i + 1], axis=0),
                    bounds_check=breg(e), oob_is_err=False))(),
                    reads=YBK + ["d0"], writes=[ky0], key="g" + ky0)
                P.dma("pool", (lambda i=i, y1=y1: lambda e: e.indirect_dma_start(
                    out=y1, out_offset=None, in_=yb_d[:, :],
                    in_offset=bass.IndirectOffsetOnAxis(ap=d1i[:, i:i + 1], axis=0),
                    bounds_check=breg(e), oob_is_err=False))(),
                    reads=YBK + ["d1"], writes=[ky1], key="g" + ky1)
                P.op("act", (lambda xq=xq: lambda e: e.activation(out=xq, in_=xq, func=AF.Copy, scale=ALPHA))(),
                     reads=[kx], writes=[kx])
                P.op("dve", (lambda i=i, y0=y0, xq=xq, aq=aq: lambda e: e.scalar_tensor_tensor(
                    out=aq, in0=y0, scalar=gate1[:, i:i + 1], in1=xq, op0=ALU.mult, op1=ALU.add))(),
                    reads=[ky0, "gate1", kx], writes=[ka])
                P.op("dve", (lambda i=i, y1=y1, aq=aq: lambda e: e.scalar_tensor_tensor(
                    out=aq, in0=y1, scalar=gate2[:, i:i + 1], in1=aq, op0=ALU.mult, op1=ALU.add))(),
                    reads=[ky1, "gate2", ka], writes=[ka])
                ln_tail(aq, ka, aq, ka, ln1g, ln1b, "ln1g", "ln1b")
                P.dma("sp", (lambda t0=t0, aq=aq: lambda e: e.dma_start(out=out_d[t0:t0 + 128, :], in_=aq))(),
                      reads=[ka], writes=["out"], key="outst%d" % q)

        final_keys = ["out"]
        P.analyze(final_wait_keys=final_keys, do_schedule=SCHEDULE)
        print("n_ops", len(P.ops), "n_waits", P.n_waits, "sems", len(P.sem_names), "sched_us", getattr(P, "sched_len", None))
        P.emit(nc, st)
    return nc


def prep_inputs(inputs):
    x = np.asarray(inputs["x"], np.float32)
    w_in = np.asarray(inputs["w_in"], np.float32)[0]
    sl = lambda a, b: w_in[:, a:b]
    w_in_r = np.ascontiguousarray(np.concatenate(
        [sl(512, 1024), sl(2304, 2560), sl(0, 512), sl(2048, 2304), sl(1024, 1536), sl(2560, 3072),
         sl(1536, 2048), sl(3088, 3600)], axis=1))
    w_gaT = np.ascontiguousarray(w_in[:, 3072:3088].T)
    shared = {
        "w_in_r": w_in_r,
        "w_gaT": w_gaT,
        "w_a2": np.ascontiguousarray(inputs["w_a2"][0]),
        "b_a": np.ascontiguousarray(inputs["b_a"][0:1]),
        "lb_logits": np.ascontiguousarray(inputs["lb_logits"]),
        "norm_hg": np.ascontiguousarray(np.concatenate([inputs["norm_h"][0], inputs["norm_g"][0]])[None, :]),
        "w_out": np.ascontiguousarray(inputs["w_out"][0]),
        "ln1_g": np.ascontiguousarray(inputs["ln1_g"][0:1]),
        "ln1_b": np.ascontiguousarray(inputs["ln1_b"][0:1]),
        "w_router": np.ascontiguousarray(np.concatenate([inputs["w_group_router"][0],
                                                         inputs["w_expert_router"][0]], axis=1)),
        "w_gu": np.ascontiguousarray(np.concatenate(
            [np.asarray(inputs["w_gate"][0]).reshape(NE, 8, 128, 512).transpose(0, 2, 1, 3),
             np.asarray(inputs["w_up"][0]).reshape(NE, 8, 128, 512).transpose(0, 2, 1, 3)], axis=2).reshape(NE, 128, 8192)),
        "w_dn": np.ascontiguousarray(np.asarray(inputs["w_down"][0]).reshape(NE, 4, 128, D).transpose(0, 2, 1, 3)
                                     .reshape(NE, 128, 4096)),
        "ln2_g": np.ascontiguousarray(inputs["ln2_g"][0:1]),
        "ln2_b": np.ascontiguousarray(inputs["ln2_b"][0:1]),
        "cst": make_consts(),
    }
    shared = {k: np.asarray(v, np.float32) for k, v in shared.items()}
    import ml_dtypes
    shared["zeros_bf"] = np.zeros((1024, D), ml_dtypes.bfloat16)
    in_maps = []
    for c in range(NCORES):
        m = dict(shared)
        m["x"] = np.ascontiguousarray(x[c])
        m["xT"] = np.ascontiguousarray(x[c].reshape(NT, 128, 8, 128).transpose(0, 3, 2, 1).reshape(NT, 128, 1024))
        in_maps.append(m)
    return in_maps


STAGE = "full"
SCHEDULE = True
VARIANT = 2
BODY_LEVEL = 9
SETUP_LEVEL = 9
NTILES = NT


def kernel(**inputs):
    in_maps = prep_inputs(inputs)
    nc = build_program(STAGE)
    if STAGE == "A":
        for m in in_maps:
            for k in ("w_gu", "w_dn"):
                m.pop(k)
    res = run_bass_kernel_spmd(nc, in_maps, core_ids=list(range(NCORES)))
    out = np.stack([np.asarray(res.results[c]["out"], np.float32) for c in range(NCORES)], axis=0)
    return out
```
